# Optimizing a Trainium2 kernel written in Bass

```python
import jax, jax.numpy as jnp
from jax import lax
import numpy as np

D_MODEL = 1024
BATCH = 8
SEQ = 8192
DEPTH = 1
DEC_BATCH = 32
DEC_SEQ = 2048
PAST_LEN = 128

MIX_WIDTH = D_MODEL
W_A = MIX_WIDTH // 2
W_B = MIX_WIDTH - W_A
HEAD_DIM = 64
N_HEADS_A = W_A // HEAD_DIM
N_HEADS_B = W_B // HEAD_DIM
CONV_A = 3
CONV_B = 31
RMS_EPS = 1e-6
LN_EPS = 1e-5
IN_COLS = 4 * W_A + 3 * W_B
SPLITS = (W_A, 2 * W_A, 3 * W_A, 4 * W_A, 4 * W_A + W_B, 4 * W_A + 2 * W_B)

kernel_name = "hybrid_gated_conv_conformer_encoder"


def rmsnorm(x, g):
    xf = x.astype(jnp.float32)
    y = xf * lax.rsqrt(jnp.mean(xf * xf, axis=-1, keepdims=True) + RMS_EPS)
    return (y * g.astype(jnp.float32)).astype(x.dtype)


def dwconv(x, w):
    k = w.shape[0]
    pad = k // 2
    return lax.conv_general_dilated(
        x, w[:, None, :].astype(x.dtype), window_strides=(1,), padding=[(pad, pad)],
        dimension_numbers=("NWC", "WIO", "NWC"), feature_group_count=x.shape[-1])


def head_layernorm(x, g, b):
    bsz, L, _ = x.shape
    xf = x.astype(jnp.float32).reshape(bsz, L, N_HEADS_B, HEAD_DIM)
    mu = jnp.mean(xf, axis=-1, keepdims=True)
    var = jnp.mean(jnp.square(xf - mu), axis=-1, keepdims=True)
    y = ((xf - mu) * lax.rsqrt(var + LN_EPS)).reshape(bsz, L, W_B)
    return (y * g.astype(jnp.float32) + b.astype(jnp.float32)).astype(x.dtype)


def mixer_layer(x, c, norm_g, w_ada, b_ada, w_in, conv_a_w, conv_b_w, conv_b_b, ln_g, ln_b, w_out):
    mod = jax.nn.silu(c) @ w_ada + b_ada
    shift, scale, gate = jnp.split(mod, 3, axis=-1)
    h = rmsnorm(x, norm_g) * (1.0 + scale[:, None, :]) + shift[:, None, :]
    p = h @ w_in
    a_in, a_b, a_c, a_z, b_v, b_g, b_z = jnp.split(p, SPLITS, axis=-1)
    y_a = a_b * dwconv(a_c * a_in, conv_a_w) * jax.nn.silu(a_z)
    u = b_v * jax.nn.sigmoid(b_g)
    u = dwconv(u, conv_b_w) + conv_b_b
    y_b = jax.nn.silu(head_layernorm(u, ln_g, ln_b)) * jax.nn.silu(b_z)
    y = jnp.concatenate([y_a, y_b], axis=-1) @ w_out
    return x + gate[:, None, :] * y


def run_trunk(x, c, norm_g, w_ada, b_ada, w_in, conv_a_w, conv_b_w, conv_b_b, ln_g, ln_b, w_out, final_g):
    for l in range(DEPTH):
        x = mixer_layer(x, c, norm_g[l], w_ada[l], b_ada[l], w_in[l], conv_a_w[l],
                        conv_b_w[l], conv_b_b[l], ln_g[l], ln_b[l], w_out[l])
    return rmsnorm(x, final_g)


def setup_inputs(seed: int = 0) -> dict:
    key = jax.random.key(seed)
    ks = jax.random.split(key, 16)
    f32 = jnp.float32
    D = D_MODEL
    x_prompt = jax.random.normal(ks[0], (BATCH, SEQ, D), f32)
    x_sample = jax.random.normal(ks[1], (DEC_BATCH, DEC_SEQ, D), f32)
    c_prompt = jax.random.normal(ks[2], (BATCH, D), f32)
    c_sample = jax.random.normal(ks[3], (DEC_BATCH, D), f32)
    norm_g = 1.0 + 0.02 * jax.random.normal(ks[4], (DEPTH, D), f32)
    w_ada = jax.random.normal(ks[5], (DEPTH, D, 3 * D), f32) * (D ** -0.5)
    b_ada = 0.02 * jax.random.normal(ks[6], (DEPTH, 3 * D), f32)
    w_in = jax.random.normal(ks[7], (DEPTH, D, IN_COLS), f32) * (D ** -0.5)
    conv_a_w = jax.random.normal(ks[8], (DEPTH, CONV_A, W_A), f32) * (CONV_A ** -0.5)
    conv_b_w = jax.random.normal(ks[9], (DEPTH, CONV_B, W_B), f32) * (CONV_B ** -0.5)
    conv_b_b = 0.02 * jax.random.normal(ks[10], (DEPTH, W_B), f32)
    ln_g = 1.0 + 0.02 * jax.random.normal(ks[11], (DEPTH, W_B), f32)
    ln_b = 0.02 * jax.random.normal(ks[12], (DEPTH, W_B), f32)
    w_out = jax.random.normal(ks[13], (DEPTH, MIX_WIDTH, D), f32) * (MIX_WIDTH ** -0.5)
    final_g = 1.0 + 0.02 * jax.random.normal(ks[14], (D,), f32)
    return {"x_prompt": x_prompt, "x_sample": x_sample, "c_prompt": c_prompt, "c_sample": c_sample,
            "norm_g": norm_g, "w_ada": w_ada, "b_ada": b_ada, "w_in": w_in,
            "conv_a_w": conv_a_w, "conv_b_w": conv_b_w, "conv_b_b": conv_b_b,
            "ln_g": ln_g, "ln_b": ln_b, "w_out": w_out, "final_g": final_g}


def reference(x_prompt, x_sample, c_prompt, c_sample, norm_g, w_ada, b_ada, w_in,
              conv_a_w, conv_b_w, conv_b_b, ln_g, ln_b, w_out, final_g):
    y_prompt = run_trunk(x_prompt, c_prompt, norm_g, w_ada, b_ada, w_in, conv_a_w,
                         conv_b_w, conv_b_b, ln_g, ln_b, w_out, final_g)
    y_sample = run_trunk(x_sample, c_sample, norm_g, w_ada, b_ada, w_in, conv_a_w,
                         conv_b_w, conv_b_b, ln_g, ln_b, w_out, final_g)
    return (y_prompt, y_sample)
```

```python
import contextlib
import numpy as np
import concourse.bass as bass
import concourse.mybir as mybir
from concourse.bass_utils import run_bass_kernel_spmd

F32 = mybir.dt.float32
BF16 = mybir.dt.bfloat16
AF = mybir.ActivationFunctionType
ALU = mybir.AluOpType

D = 1024
NT = 512
WA_ = 512
INC = 3584
KB = 31
KA = 3
RMS_EPS = 1e-6
LN_EPS = 1e-5
N_CORES = 8
SEQ_TILES = (16, 4, 4, 4, 4)


class Tracker:
    ENGINES = ("tensor", "vector", "scalar", "gpsimd", "sync")

    def __init__(self, nc, stack):
        self.nc = nc
        self.stack = stack
        self.sems = {}
        self.count = {}
        self.streams = {e: [] for e in self.ENGINES}
        self.waited = {e: {} for e in self.ENGINES}
        self.lastw = {}
        self.readers = {}
        self.pending = {e: [] for e in self.ENGINES}
        self.nsem = 0
        self.clock = 0
        self.touch = {}
        for e in ("tensor", "vector", "scalar", "gpsimd"):
            self._sem(e)

    def _sem(self, key):
        if key not in self.sems:
            self.nsem += 1
            self.sems[key] = self.stack.enter_context(self.nc.semaphore("sm%d" % self.nsem))
            self.count[key] = 0
        return self.sems[key]

    def _need(self, eng, evs):
        need = {}
        for ev in evs:
            if ev is None:
                continue
            k, v = ev
            if v > need.get(k, 0):
                need[k] = v
        for k, v in need.items():
            if self.waited[eng].get(k, 0) >= v:
                continue
            self.waited[eng][k] = v
            sem = self.sems[k]
            self.streams[eng].append(lambda e, sem=sem, v=v: e.wait_ge(sem, v))

    def _deps(self, eng, reads, writes):
        evs = []
        for b in reads:
            evs.append(self.lastw.get(b))
        for b in writes:
            evs.append(self.lastw.get(b))
            evs.extend(self.readers.get(b, ()))
        return evs

    def _touch(self, reads, writes):
        self.clock += 1
        for b in reads:
            self.touch[b] = self.clock
        for b in writes:
            self.touch[b] = self.clock

    def op(self, eng, fn, reads=(), writes=(), signal=True):
        self._touch(reads, writes)
        self._need(eng, self._deps(eng, reads, writes))
        self.pending[eng].append((tuple(reads), tuple(writes)))
        if signal:
            self.count[eng] += 1
            v = self.count[eng]
            sem = self.sems[eng]
            self.streams[eng].append(lambda e, fn=fn, sem=sem: fn(e).then_inc(sem, 1))
            ev = (eng, v)
            for rs, ws in self.pending[eng]:
                for b in rs:
                    self.readers.setdefault(b, []).append(ev)
                for b in ws:
                    self.lastw[b] = ev
                    self.readers[b] = []
            self.pending[eng] = []
        else:
            self.streams[eng].append(lambda e, fn=fn: fn(e))

    def dma(self, eng, fn, reads=(), writes=(), semkey=None):
        self._touch(reads, writes)
        self._sem(semkey)
        self._need(eng, self._deps(eng, reads, writes))
        self.count[semkey] += 16
        v = self.count[semkey]
        sem = self.sems[semkey]
        self.streams[eng].append(lambda e, fn=fn, sem=sem: fn(e).then_inc(sem, 16))
        ev = (semkey, v)
        for b in reads:
            self.readers.setdefault(b, []).append(ev)
        for b in writes:
            self.lastw[b] = ev
            self.readers[b] = []
        return ev

    def emit(self):
        with self.nc.Block() as block:
            for name in self.ENGINES:
                stream = self.streams[name]
                if not stream:
                    continue

                def body(e, stream=stream):
                    for f in stream:
                        f(e)

                getattr(block, name)(body)


class Ring:
    def __init__(self, name, bufs, tracker):
        self.name, self.bufs, self.t = name, bufs, tracker
        self.pinned = set()

    def next(self, pin=False):
        cands = [j for j in range(len(self.bufs)) if j not in self.pinned]
        j = min(cands, key=lambda j: self.t.touch.get((self.name, j), -1 + 0.001 * j))
        self.t.clock += 1
        self.t.touch[(self.name, j)] = self.t.clock
        if pin:
            self.pinned.add(j)
        return self.bufs[j], (self.name, j)

    def unpin(self, key):
        self.pinned.discard(key[1])


def build_nc(seq_tiles=SEQ_TILES):
    nseq = len(seq_tiles)
    ntiles = sum(seq_tiles)
    tile_seq, tile_first, tile_last = [], [], []
    for s, n in enumerate(seq_tiles):
        for t in range(n):
            tile_seq.append(s)
            tile_first.append(t == 0)
            tile_last.append(t == n - 1)

    nc = bass.Bass("TRN2", target_bir_lowering=False)
    x_d = nc.dram_tensor("x", [ntiles * NT, D], F32, kind="ExternalInput").ap()
    cT_d = nc.dram_tensor("cT", [D, nseq], F32, kind="ExternalInput").ap()
    ng_d = nc.dram_tensor("ngT", [128, 8], F32, kind="ExternalInput").ap()
    wada_d = nc.dram_tensor("w_ada", [D, 3 * D], F32, kind="ExternalInput").ap()
    bada_d = nc.dram_tensor("b_ada", [1, 3 * D], F32, kind="ExternalInput").ap()
    win_d = nc.dram_tensor("w_in", [D, INC], F32, kind="ExternalInput").ap()
    ca_d = nc.dram_tensor("caT", [128, 4 * KA], F32, kind="ExternalInput").ap()
    wsel_d = nc.dram_tensor("wsel", [128, 4 * 2 * 16], F32, kind="ExternalInput").ap()
    cbb_d = nc.dram_tensor("cbbT", [128, 4], F32, kind="ExternalInput").ap()
    lng_d = nc.dram_tensor("lngT", [128, 4], F32, kind="ExternalInput").ap()
    lnb_d = nc.dram_tensor("lnbT", [128, 4], F32, kind="ExternalInput").ap()
    wout_d = nc.dram_tensor("w_out", [D, D], F32, kind="ExternalInput").ap()
    fg_d = nc.dram_tensor("fg", [1, D], F32, kind="ExternalInput").ap()
    y_d = nc.dram_tensor("y", [ntiles * NT, D], F32, kind="ExternalOutput").ap()
    gate_d = nc.dram_tensor("gate_scratch", [nseq, D], F32).ap()

    with contextlib.ExitStack() as st:
        def sb(name, shape, dt):
            return st.enter_context(nc.sbuf_tensor(name, shape, dt))

        def ps(name, shape, dt):
            return st.enter_context(nc.psum_tensor(name, shape, dt))

        T = Tracker(nc, st)

        w_in_sb = sb("w_in_sb", [128, 8, INC], BF16)
        w_out_sb = sb("w_out_sb", [128, 8, D], BF16)
        WC = sb("WC", [128, 4, 2, 16, 64], BF16)
        Cc2 = sb("Cc2", [128, 64], F32)
        wsel = sb("wsel_sb", [128, 4, 2, 16], F32)
        caH = sb("caH", [128, 4 * KA], F32)
        Bm = sb("Bm", [128, 128], BF16)
        ident = sb("ident", [128, 128], BF16)
        identf = sb("identf", [128, 128], F32)
        Cm = sb("Cm", [128, 128], F32)
        ones = sb("ones", [128, 8], F32)
        mhalf = sb("mhalf", [128, 4], F32)
        cT_sb = sb("cT_sb", [128, 8, nseq], F32)
        thc = sb("thc", [128, 8, nseq], F32)
        scT = sb("scT", [128, 8, nseq], F32)
        ngT = sb("ngT_sb", [128, 8], F32)
        caT = sb("caT_sb", [128, 4 * KA], F32)
        cbb = sb("cbb_sb", [128, 4], F32)
        lng = sb("lng_sb", [128, 4], F32)
        lnb = sb("lnb_sb", [128, 4], F32)
        qg = sb("qg", [128, 4], F32)
        qb = sb("qb", [128, 4], F32)
        cbs = sb("cbs", [128, 4], F32)
        gsT = sb("gsT", [128, 8, nseq], F32)
        shT = sb("shT", [128, 8, nseq], F32)
        fg_bc = sb("fg_bc", [128, D], F32)
        gate_bc = [sb("gate_bc%d" % i, [128, D], F32) for i in range(2)]
        xs = sb("xs", [128, 4, D], BF16)
        gb2 = sb("gb2", [128, 2, NT], BF16)
        hT = sb("hT", [128, 8, NT], BF16)
        yT = sb("yT", [128, 8, NT], BF16)
        u_sb = [sb("u%d" % i, [128, 4, 2, NT + 30], BF16) for i in range(2)]
        t1_sb = [sb("t1%d" % i, [128, 4, NT + 2], BF16) for i in range(2)]
        ga = sb("ga", [128, 4, NT], BF16)
        gb = sb("gb", [128, 4, NT], BF16)
        sqr = Ring("sq", [sb("sq%d" % i, [128, NT], BF16) for i in range(2)], T)
        ssA = sb("ssA", [128, 4], F32)
        msA = sb("msA", [128, 4], F32)
        rsA = sb("rsA", [128, 4], F32)
        ssE = sb("ssE", [128, 8], F32)
        msE = sb("msE", [128, 8], F32)
        rsE = sb("rsE", [128, 8], F32)
        XB = Ring("xb", [sb("xb%d" % i, [128, D], F32) for i in range(8)], T)
        TP = Ring("tp", [sb("tp%d" % i, [128, NT], F32) for i in range(5)], T)
        PB = Ring("pb", [ps("pb%d" % i, [128, NT], F32) for i in range(6)], T)
        ptr_t = [ps("ptr%d" % i, [128, 2 * NT], BF16) for i in range(2)]
        PTR = Ring("ptr", [ptr_t[0][:, 0:NT], ptr_t[1][:, 0:NT]], T)

        cur = {"eidx": 0}

        def mk_ident(t, key):
            T.op("gpsimd", lambda e: e.memset(t[:], 0.0), writes=[key])
            T.op("gpsimd", lambda e: e.affine_select(out=t[:], in_=t[:], compare_op=ALU.not_equal, fill=1.0,
                                                     base=0, pattern=[[-1, 128]], channel_multiplier=1),
                 reads=[key], writes=[key])

        wcnt = {"n": 0}

        def load_w_block(kind, blk):
            for kc in range(8):
                load_w_piece(kind, blk, kc)

        def load_w_piece(kind, blk, kc):
            if True:
                stg, skey = TP.next()
                if kind == "in":
                    src = win_d[kc * 128:(kc + 1) * 128, blk * NT:(blk + 1) * NT]
                    dst = w_in_sb[:, kc, blk * NT:(blk + 1) * NT]
                    wkey = ("w_in", blk)
                else:
                    src = wout_d[kc * 128:(kc + 1) * 128, blk * NT:(blk + 1) * NT]
                    dst = w_out_sb[:, kc, blk * NT:(blk + 1) * NT]
                    wkey = ("w_out", blk)
                T.dma("sync", lambda e, stg=stg, src=src: e.dma_start(out=stg[:], in_=src), writes=[skey], semkey=("tpl", skey[1]))
                wcnt["n"] += 1
                if wcnt["n"] % 2 == 0:
                    T.op("vector", lambda e, stg=stg, dst=dst: e.tensor_copy(out=dst, in_=stg[:]), reads=[skey], writes=[wkey])
                else:
                    T.op("scalar", lambda e, stg=stg, dst=dst: e.activation(out=dst, in_=stg[:], func=AF.Copy), reads=[skey], writes=[wkey])

        mk_ident(ident, "ident")
        mk_ident(identf, "identf")

        def mk_block(t, key, val, diag=None):
            T.op("gpsimd", lambda e: e.memset(t[:], val), writes=[key])
            T.op("gpsimd", lambda e: e.affine_select(out=t[:, 0:64], in_=t[:, 0:64], compare_op=ALU.is_ge, fill=0.0,
                                                     base=63, pattern=[[0, 64]], channel_multiplier=-1),
                 reads=[key], writes=[key])
            T.op("gpsimd", lambda e: e.affine_select(out=t[:, 64:128], in_=t[:, 64:128], compare_op=ALU.is_ge, fill=0.0,
                                                     base=-64, pattern=[[0, 64]], channel_multiplier=1),
                 reads=[key], writes=[key])
            if diag is not None:
                T.op("gpsimd", lambda e: e.affine_select(out=t[:], in_=t[:], compare_op=ALU.not_equal, fill=diag,
                                                         base=0, pattern=[[-1, 128]], channel_multiplier=1),
                     reads=[key], writes=[key])

        mk_block(Cm, "Cm", -1.0 / 64, diag=63.0 / 64)
        mk_block(Bm, "Bm", 1.0 / 64)
        T.op("gpsimd", lambda e: e.tensor_tensor(out=Cc2[:], in0=Cm[:, 0:64], in1=Cm[:, 64:128], op=ALU.add), reads=["Cm"], writes=["Cc2"])
        for i_ in range(2):
            T.op("gpsimd", lambda e, i_=i_: e.memset(u_sb[i_][:], 0.0), writes=[("u", i_)] + [("us", i_, j_) for j_ in range(4)])
        T.op("gpsimd", lambda e: e.memset(ones[:], 1.0), writes=["ones"])
        T.op("gpsimd", lambda e: e.memset(mhalf[:], -0.5), writes=["mhalf"])

        small = [(cT_sb, cT_d.rearrange("(k p) s -> p k s", p=128), "cT"), (ngT, ng_d, "ngT"), (caT, ca_d, "caT"),
                 (wsel, wsel_d.rearrange("p (j h q) -> p j h q", j=4, h=2), "wsel"), (cbb, cbb_d, "cbb"), (lng, lng_d, "lng"), (lnb, lnb_d, "lnb")]
        for t, d, key in small:
            T.dma("sync", lambda e, t=t, d=d: e.dma_start(out=t[:], in_=d), writes=[key], semkey=("ld", key))
        bstage = [(fg_bc, "fg_bc"), (gate_bc[0], ("gate_bc", 0)), (gate_bc[1], ("gate_bc", 1))]
        for q, (t, key) in enumerate(bstage):
            T.dma("sync", lambda e, t=t, q=q: e.dma_start(out=t[0:1, :], in_=bada_d[0:1, q * D:(q + 1) * D]),
                  writes=[key], semkey=("ldb", q))

        T.op("scalar", lambda e: e.activation(out=thc[:], in_=cT_sb[:], func=AF.Tanh, scale=0.5), reads=["cT"], writes=["thc"])
        T.op("vector", lambda e: e.scalar_tensor_tensor(out=scT[:], in0=thc[:], scalar=1.0, in1=cT_sb[:], op0=ALU.add, op1=ALU.mult),
             reads=["thc", "cT"], writes=["scT"])
        T.op("vector", lambda e: e.tensor_scalar(out=scT[:], in0=scT[:], scalar1=0.5, scalar2=None, op0=ALU.mult),
             reads=["scT"], writes=["scT"])

        def a_load(i):
            row0 = i * NT
            for sub in range(4):
                xbuf, xkey = XB.next(pin=True)
                T.dma("sync", lambda e, xbuf=xbuf, r=row0 + sub * 128: e.dma_start(out=xbuf[:], in_=x_d[r:r + 128, :]),
                      writes=[xkey], semkey=("xl", xkey[1]))
                a_prep.bufs[sub] = (xbuf, xkey)

        def a_compute(i):
            for sub in range(4):
                xbuf, xkey = a_prep.bufs[sub]
                T.op("scalar", lambda e, xbuf=xbuf, sub=sub: e.activation(out=xs[:, sub, :], in_=xbuf[:], func=AF.Square,
                                                                          accum_out=ssA[:, sub:sub + 1]),
                     reads=[xkey], writes=[("xs", sub), "ssA"])
            T.op("vector", lambda e: e.tensor_scalar(out=msA[:], in0=ssA[:], scalar1=1.0 / D, scalar2=RMS_EPS, op0=ALU.mult, op1=ALU.add),
                 reads=["ssA"], writes=["msA"])
            T.op("gpsimd", lambda e: e.tensor_tensor(out=rsA[:], in0=msA[:], in1=mhalf[:], op=ALU.pow), reads=["msA", "mhalf"], writes=["rsA"])
            for sub in range(4):
                xbuf, xkey = a_prep.bufs[sub]
                T.op("gpsimd", lambda e, xbuf=xbuf, sub=sub: e.tensor_scalar(out=xs[:, sub, :], in0=xbuf[:], scalar1=rsA[:, sub:sub + 1],
                                                                             scalar2=1.0, op0=ALU.mult, op1=ALU.mult),
                     reads=[xkey, "rsA"], writes=[("xs", sub)])
                XB.unpin(xkey)

        def a_prep(i):
            a_load(i)
            a_compute(i)
        a_prep.bufs = [None] * 4

        def mod_thirds(thirds, between):
            banks = {}
            for third in thirds:
                banks[third] = [PB.next(pin=True), PB.next(pin=True)]
            for kc in range(8):
                for third in thirds:
                    xbuf, xkey = XB.next()
                    T.dma("sync", lambda e, xbuf=xbuf, kc=kc, third=third: e.dma_start(
                        out=xbuf[:], in_=wada_d[kc * 128:(kc + 1) * 128, third * D:(third + 1) * D]),
                        writes=[xkey], semkey=("xl", xkey[1]))
                    for h in range(2):
                        bank, bkey = banks[third][h]
                        T.op("tensor", lambda e, bank=bank, xbuf=xbuf, kc=kc, h=h: e.matmul(
                            bank[0:nseq, :], lhsT=scT[:, kc, :], rhs=xbuf[:, h * NT:(h + 1) * NT], start=(kc == 0), stop=False),
                            reads=["scT", xkey], writes=[bkey], signal=(h == 1))
                    between()
            out = {}
            for third in thirds:
                t, key = bstage[third]
                for h in range(2):
                    bank, bkey = banks[third][h]
                    T.op("tensor", lambda e, bank=bank, t=t, h=h: e.matmul(
                        bank[0:nseq, :], lhsT=ones[0:1, 0:nseq], rhs=t[0:1, h * NT:(h + 1) * NT], start=False, stop=True),
                        reads=["ones", key], writes=[bkey])
            for third in thirds:
                mbuf, mkey = XB.next()
                for h in range(2):
                    bank, bkey = banks[third][h]
                    T.op("vector", lambda e, mbuf=mbuf, bank=bank, h=h: e.tensor_copy(
                        out=mbuf[0:nseq, h * NT:(h + 1) * NT], in_=bank[0:nseq, :]), reads=[bkey], writes=[mkey])
                    PB.unpin(bkey)
                out[third] = (mbuf, mkey)
            return out

        first_pieces = [("in", blk, kc) for blk in (5, 4, 0, 2) for kc in range(8)]

        def between():
            for _ in range(2):
                if first_pieces:
                    load_w_piece(*first_pieces.pop(0))

        a_prep(0)
        mb = mod_thirds([0, 1], between)
        while first_pieces:
            load_w_piece(*first_pieces.pop(0))
        modbuf = [mb[0], mb[1], None]
        T.dma("sync", lambda e: e.dma_start(out=fg_bc[:], in_=fg_d.partition_broadcast(128)), writes=["fg_bc"], semkey="ld_fg")

        def gate_part():
            g = mod_thirds([2], lambda: None)[2]
            T.dma("sync", lambda e: e.dma_start(out=gate_d, in_=g[0][0:nseq, :]), reads=[g[1]], writes=["gate_d"], semkey="st_gate")
            load_gate(0)

        bank_sh, bkey_sh = PB.next()
        bank_sc, bkey_sc = PB.next()
        for kc in range(8):
            T.op("tensor", lambda e, kc=kc: e.transpose(out=bank_sh[:, kc * nseq:(kc + 1) * nseq],
                                                        in_=modbuf[0][0][0:nseq, kc * 128:(kc + 1) * 128],
                                                        identity=identf[0:nseq, 0:nseq]),
                 reads=[modbuf[0][1], "identf"], writes=[bkey_sh], signal=(kc == 7))
        for kc in range(8):
            T.op("tensor", lambda e, kc=kc: e.transpose(out=bank_sc[:, kc * nseq:(kc + 1) * nseq],
                                                        in_=modbuf[1][0][0:nseq, kc * 128:(kc + 1) * 128],
                                                        identity=identf[0:nseq, 0:nseq]),
                 reads=[modbuf[1][1], "identf"], writes=[bkey_sc], signal=(kc == 7))
        T.op("vector", lambda e: e.tensor_copy(out=shT[:].rearrange("p k s -> p (k s)"), in_=bank_sh[:, 0:8 * nseq]),
             reads=[bkey_sh], writes=["shT"])
        for kc in range(8):
            T.op("vector", lambda e, kc=kc: e.tensor_scalar(out=gsT[:, kc, :], in0=bank_sc[:, kc * nseq:(kc + 1) * nseq],
                                                            scalar1=1.0, scalar2=ngT[:, kc:kc + 1], op0=ALU.add, op1=ALU.mult),
                 reads=[bkey_sc, "ngT"], writes=["gsT"])

        bank_cb, bkey_cb = PB.next()
        T.op("tensor", lambda e: e.matmul(bank_cb[:, 0:4], lhsT=Cm[:], rhs=cbb[:], start=True, stop=True),
             reads=["Cm", "cbb"], writes=[bkey_cb])
        T.op("vector", lambda e: e.tensor_copy(out=cbs[:], in_=bank_cb[:, 0:4]), reads=[bkey_cb], writes=["cbs"])
        T.op("vector", lambda e: e.tensor_scalar(out=caH[:], in0=caT[:], scalar1=0.5, scalar2=None, op0=ALU.mult), reads=["caT"], writes=["caH"])
        T.op("vector", lambda e: e.tensor_scalar(out=qg[:], in0=lng[:], scalar1=0.25, scalar2=None, op0=ALU.mult), reads=["lng"], writes=["qg"])
        T.op("vector", lambda e: e.tensor_scalar(out=qb[:], in0=lnb[:], scalar1=0.25, scalar2=None, op0=ALU.mult), reads=["lnb"], writes=["qb"])

        def build_conv_w():
            n_w = 0
            for j in range(4):
                for h in range(2):
                    for q in range(16):
                        eng = "vector" if (n_w % 2 == 0) else "gpsimd"
                        n_w += 1
                        T.op(eng, lambda e, j=j, h=h, q=q: e.tensor_scalar(out=WC[:, j, h, q, :], in0=Cc2[:], scalar1=wsel[:, j, h, q:q + 1],
                                                                           scalar2=0.5, op0=ALU.mult, op1=ALU.mult),
                             reads=["Cc2", "wsel"], writes=[("WC", j, h, q)])

        def load_gate(s):
            g = gate_bc[s % 2]
            T.dma("sync", lambda e, g=g, s=s: e.dma_start(out=g[:], in_=gate_d[s:s + 1, :].partition_broadcast(128)),
                  reads=["gate_d"], writes=[("gate_bc", s % 2)], semkey=("ld_gate", s % 2))

        def a_T(i):
            for kc in range(8):
                a_T_group(i, kc)

        def a_T_group(i, kc):
            s = tile_seq[i]
            if True:
                pt, pkey = PTR.next()
                for sub in range(4):
                    T.op("tensor", lambda e, pt=pt, sub=sub, kc=kc: e.transpose(out=pt[:, sub * 128:(sub + 1) * 128],
                                                                                in_=xs[:, sub, kc * 128:(kc + 1) * 128],
                                                                                identity=ident[:]),
                         reads=[("xs", sub), "ident"], writes=[pkey], signal=(sub == 3))
                if kc % 2 == 0:
                    T.op("scalar", lambda e, pt=pt, kc=kc, s=s: e.activation(out=hT[:, kc, :], in_=pt, func=AF.Identity,
                                                                            scale=gsT[:, kc, s:s + 1], bias=shT[:, kc, s:s + 1]),
                         reads=[pkey, "gsT", "shT"], writes=[("hT", kc)])
                else:
                    T.op("vector", lambda e, pt=pt, kc=kc, s=s: e.tensor_scalar(out=hT[:, kc, :], in0=pt, scalar1=gsT[:, kc, s:s + 1],
                                                                               scalar2=shT[:, kc, s:s + 1], op0=ALU.mult, op1=ALU.add),
                         reads=[pkey, "gsT", "shT"], writes=[("hT", kc)])

        hT_keys = [("hT", kc) for kc in range(8)]

        def mm1(col0):
            bank, bkey = PB.next()
            for kc in range(8):
                T.op("tensor", lambda e, bank=bank, kc=kc: e.matmul(bank[:], lhsT=w_in_sb[:, kc, col0:col0 + 128], rhs=hT[:, kc, :],
                                                                    start=(kc == 0), stop=(kc == 7)),
                     reads=[("w_in", col0 // NT), ("hT", kc)], writes=[bkey], signal=(kc == 7))
            return bank, bkey

        def b1(i):
            slot = i % 2
            u, t1 = u_sb[slot], t1_sb[slot]
            for j in range(4):
                bank, bkey = mm1(2560 + j * 128)
                th, tkey = TP.next()
                T.op("scalar", lambda e, th=th, bank=bank: e.activation(out=th[:], in_=bank[:], func=AF.Tanh, scale=0.5),
                     reads=[bkey], writes=[tkey])
                bank2, bkey2 = mm1(2048 + j * 128)
                for h in range(2):
                    T.op("vector", lambda e, th=th, bank2=bank2, j=j, h=h: e.scalar_tensor_tensor(
                        out=u[64 * h:64 * h + 64, j, h, 15:15 + NT], in0=th[64 * h:64 * h + 64, :], scalar=1.0,
                        in1=bank2[64 * h:64 * h + 64, :], op0=ALU.add, op1=ALU.mult),
                        reads=[tkey, bkey2], writes=[("u", slot)])
            for j in range(4):
                bank, bkey = mm1(0 + j * 128)
                ain, akey = TP.next()
                T.op("scalar", lambda e, ain=ain, bank=bank: e.activation(out=ain[:], in_=bank[:], func=AF.Copy), reads=[bkey], writes=[akey])
                bank2, bkey2 = mm1(1024 + j * 128)
                T.op("vector", lambda e, ain=ain, bank2=bank2, j=j: e.tensor_tensor(out=t1[:, j, 1:1 + NT], in0=bank2[:], in1=ain[:], op=ALU.mult),
                     reads=[akey, bkey2], writes=[("t1", slot)])
            up, t1p = u_sb[1 - slot], t1_sb[1 - slot]
            if tile_first[i]:
                T.op("gpsimd", lambda e: e.memset(u[:, :, :, 0:15], 0.0), writes=[("u", slot)])
                T.op("gpsimd", lambda e: e.memset(t1[:, :, 0:1], 0.0), writes=[("t1", slot)])
            else:
                T.op("gpsimd", lambda e: e.tensor_copy(out=u[:, :, :, 0:15], in_=up[:, :, :, NT:NT + 15]), reads=[("u", 1 - slot)], writes=[("u", slot)])
                T.op("gpsimd", lambda e: e.tensor_copy(out=up[:, :, :, NT + 15:NT + 30], in_=u[:, :, :, 15:30]), reads=[("u", slot)], writes=[("u", 1 - slot)])
                T.op("gpsimd", lambda e: e.tensor_copy(out=t1[:, :, 0:1], in_=t1p[:, :, NT:NT + 1]), reads=[("t1", 1 - slot)], writes=[("t1", slot)])
                T.op("gpsimd", lambda e: e.tensor_copy(out=t1p[:, :, NT + 1:NT + 2], in_=t1[:, :, 1:2]), reads=[("t1", slot)], writes=[("t1", 1 - slot)])
            if tile_last[i]:
                T.op("gpsimd", lambda e: e.memset(u[:, :, :, NT + 15:NT + 30], 0.0), writes=[("u", slot)])
                T.op("gpsimd", lambda e: e.memset(t1[:, :, NT + 1:NT + 2], 0.0), writes=[("t1", slot)])

        def b2_a(j):
            bank, bkey = mm1(1536 + j * 128)
            th, tkey = TP.next()
            T.op("scalar", lambda e, th=th, bank=bank: e.activation(out=th[:], in_=bank[:], func=AF.Tanh, scale=0.5), reads=[bkey], writes=[tkey])
            T.op("vector", lambda e, th=th, bank=bank: e.scalar_tensor_tensor(out=th[:], in0=th[:], scalar=1.0, in1=bank[:], op0=ALU.add, op1=ALU.mult),
                 reads=[tkey, bkey], writes=[tkey])
            bank2, bkey2 = mm1(512 + j * 128)
            T.op("vector", lambda e, th=th, bank2=bank2, j=j: e.tensor_tensor(out=ga[:, j, :], in0=bank2[:], in1=th[:], op=ALU.mult),
                 reads=[tkey, bkey2], writes=[("ga", j)])

        def gbuf(i, j):
            if j < 2 or i % 2 == 0:
                return gb[:, j, :], ("gb", j)
            return gb2[:, j - 2, :], ("gb2", j)

        def b2_b(i, j):
            gdst, gkey = gbuf(i, j)
            bank, bkey = mm1(3072 + j * 128)
            th, tkey = TP.next()
            T.op("scalar", lambda e, th=th, bank=bank: e.activation(out=th[:], in_=bank[:], func=AF.Tanh, scale=0.5), reads=[bkey], writes=[tkey])
            T.op("vector", lambda e, th=th, bank=bank: e.scalar_tensor_tensor(out=gdst, in0=th[:], scalar=1.0, in1=bank[:], op0=ALU.add, op1=ALU.mult),
                 reads=[tkey, bkey], writes=[gkey])

        def conv_a(i, j):
            slot = i % 2
            t1 = t1_sb[slot]
            acc, akey = TP.next()
            T.op("vector", lambda e: e.tensor_scalar(out=acc[:], in0=t1[:, j, 0:NT], scalar1=caH[:, j * KA:j * KA + 1], scalar2=None, op0=ALU.mult),
                 reads=[("t1", slot), "caH"], writes=[akey])
            for k in (1, 2):
                T.op("vector", lambda e, k=k: e.scalar_tensor_tensor(out=acc[:], in0=t1[:, j, k:k + NT], scalar=caH[:, j * KA + k:j * KA + k + 1],
                                                                   in1=acc[:], op0=ALU.mult, op1=ALU.add),
                     reads=[("t1", slot), "caH", akey], writes=[akey])
            T.op("vector", lambda e: e.tensor_tensor(out=yT[:, j, :], in0=acc[:], in1=ga[:, j, :], op=ALU.mult),
                 reads=[akey, ("ga", j)], writes=[("yT", j)])

        def us_dma(i):
            slot = i % 2
            u = u_sb[slot]
            for h in range(2):
                a, b = 64 * h, 64 * (1 - h)
                for j in range(4):
                    T.dma("sync", lambda e, a=a, b=b, h=h, j=j: e.dma_start(out=u[b:b + 64, j, h, 0:NT + 29], in_=u[a:a + 64, j, h, 1:NT + 30]),
                          reads=[("u", slot)], writes=[("us", slot, j)], semkey=("usd", slot, h, j))

        def conv_b(i, j):
            slot = i % 2
            u = u_sb[slot]
            bank, bkey = PB.next()
            for q in range(16):
                for h in range(2):
                    T.op("tensor", lambda e, bank=bank, q=q, h=h: e.matmul(
                        bank[64 * h:64 * h + 64, :], lhsT=WC[:, j, h, q, :], rhs=u[:, j, h, 2 * q:2 * q + NT],
                        start=(q == 0), stop=(q == 15), tile_position=(0, 64 * h)),
                        reads=[("WC", j, h, q), ("u", slot), ("us", slot, j)], writes=[bkey], signal=(q == 15 and h == 1))
            sq, skey = sqr.next()
            T.op("scalar", lambda e, sq=sq, bank=bank: e.activation(out=sq[:], in_=bank[:], func=AF.Square, bias=cbs[:, j:j + 1], scale=1.0),
                 reads=[bkey, "cbs"], writes=[skey])
            cc, ckey = TP.next(pin=True)
            T.op("scalar", lambda e, cc=cc, bank=bank: e.activation(out=cc[:], in_=bank[:], func=AF.Identity, bias=cbs[:, j:j + 1], scale=1.0),
                 reads=[bkey, "cbs"], writes=[ckey])
            return cc, ckey, sq, skey

        def ln_p1(ti, items):
            st_ = []
            for (j, cc, ckey, sq, skey) in items:
                vb, vkey = PB.next()
                T.op("tensor", lambda e, vb=vb, sq=sq: e.matmul(vb[:], lhsT=Bm[:], rhs=sq[:], start=True, stop=True), reads=["Bm", skey], writes=[vkey])
                r, rkey = TP.next(pin=True)
                st_.append((j, cc, ckey, vb, vkey, r, rkey))
            for (j, cc, ckey, vb, vkey, r, rkey) in st_:
                T.op("scalar", lambda e, r=r, vb=vb: e.activation(out=r[:], in_=vb[:], func=AF.Ln, bias=LN_EPS, scale=1.0), reads=[vkey], writes=[rkey])
                T.op("scalar", lambda e, r=r: e.activation(out=r[:], in_=r[:], func=AF.Exp, scale=-0.5), reads=[rkey], writes=[rkey])
            return st_

        def ln_p2(ti, st_):
            for (j, cc, ckey, vb, vkey, r, rkey) in st_:
                T.op("vector", lambda e, r=r, cc=cc: e.tensor_tensor(out=r[:], in0=cc[:], in1=r[:], op=ALU.mult),
                     reads=[ckey, rkey], writes=[rkey])
                T.op("vector", lambda e, r=r, j=j: e.tensor_scalar(out=r[:], in0=r[:], scalar1=qg[:, j:j + 1], scalar2=qb[:, j:j + 1], op0=ALU.mult, op1=ALU.add),
                     reads=[rkey, "qg", "qb"], writes=[rkey])

        def ln_p3(ti, st_):
            for (j, cc, ckey, vb, vkey, r, rkey) in st_:
                T.op("scalar", lambda e, cc=cc, r=r: e.activation(out=cc[:], in_=r[:], func=AF.Tanh, scale=2.0), reads=[rkey], writes=[ckey])

        def ln_p4(ti, st_):
            for (j, cc, ckey, vb, vkey, r, rkey) in st_:
                T.op("vector", lambda e, cc=cc, r=r: e.scalar_tensor_tensor(out=cc[:], in0=cc[:], scalar=1.0, in1=r[:], op0=ALU.add, op1=ALU.mult),
                     reads=[ckey, rkey], writes=[ckey])
                TP.unpin(rkey)
                gsrc, gkey = gbuf(ti, j)
                T.op("gpsimd", lambda e, cc=cc, j=j, gsrc=gsrc: e.tensor_tensor(out=yT[:, 4 + j, :], in0=cc[:], in1=gsrc, op=ALU.mult),
                     reads=[ckey, gkey], writes=[("yT", 4 + j)])
                TP.unpin(ckey)

        def ln_pair(ti, items):
            st_ = ln_p1(ti, items)
            ln_p2(ti, st_)
            ln_p3(ti, st_)
            ln_p4(ti, st_)

        def d_loads(i):
            row0 = i * NT
            for sub in range(4):
                xbuf, xkey = XB.next(pin=True)
                T.dma("sync", lambda e, xbuf=xbuf, r=row0 + sub * 128: e.dma_start(out=xbuf[:], in_=x_d[r:r + 128, :]),
                      writes=[xkey], semkey=("xl", xkey[1]))
                d_loads.bufs[sub] = (xbuf, xkey)
        d_loads.bufs = [None] * 4

        def stage_d(i, nxt=None):
            s = tile_seq[i]
            g = gate_bc[s % 2]
            gkey = ("gate_bc", s % 2)
            row0 = i * NT
            fin = []
            for sub in range(4):
                xbuf, xkey = d_loads.bufs[sub]
                for h in range(2):
                    if nxt is not None and sub < 2:
                        a_T_group(nxt, (sub * 2 + h) * 2)
                        a_T_group(nxt, (sub * 2 + h) * 2 + 1)
                    bank, bkey = PB.next()
                    for kc in range(8):
                        T.op("tensor", lambda e, bank=bank, kc=kc, sub=sub, h=h: e.matmul(
                            bank[:], lhsT=yT[:, kc, sub * 128:(sub + 1) * 128], rhs=w_out_sb[:, kc, h * NT:(h + 1) * NT],
                            start=(kc == 0), stop=(kc == 7)),
                            reads=[("yT", kc), ("w_out", h)], writes=[bkey], signal=(kc == 7))
                    tmp, tkey = TP.next()
                    T.op("vector", lambda e, tmp=tmp, bank=bank, h=h: e.tensor_tensor(out=tmp[:], in0=bank[:], in1=g[:, h * NT:(h + 1) * NT], op=ALU.mult),
                         reads=[bkey, gkey], writes=[tkey])
                    T.op("gpsimd", lambda e, tmp=tmp, xbuf=xbuf, h=h: e.tensor_tensor(out=xbuf[:, h * NT:(h + 1) * NT], in0=tmp[:],
                                                                                      in1=xbuf[:, h * NT:(h + 1) * NT], op=ALU.add),
                         reads=[tkey, xkey], writes=[xkey])
                fin.append((xbuf, xkey, row0 + sub * 128))
                if len(fin) > 1:
                    d_finish(*fin.pop(0))
            while fin:
                d_finish(*fin.pop(0))

        def d_finish(xbuf, xkey, r):
            c = cur["eidx"] % 8
            cur["eidx"] += 1
            jb, jkey = TP.next()
            T.op("scalar", lambda e: e.activation(out=jb[:].bitcast(BF16), in_=xbuf[:], func=AF.Square, accum_out=ssE[:, c:c + 1]),
                 reads=[xkey], writes=[jkey, ("ssE", c)])
            T.op("vector", lambda e: e.tensor_scalar(out=msE[:, c:c + 1], in0=ssE[:, c:c + 1], scalar1=1.0 / D, scalar2=RMS_EPS,
                                                     op0=ALU.mult, op1=ALU.add), reads=[("ssE", c)], writes=[("msE", c)])
            T.op("gpsimd", lambda e: e.tensor_tensor(out=rsE[:, c:c + 1], in0=msE[:, c:c + 1], in1=mhalf[:, 0:1], op=ALU.pow),
                 reads=[("msE", c), "mhalf"], writes=[("rsE", c)])
            T.op("vector", lambda e: e.scalar_tensor_tensor(out=xbuf[:], in0=xbuf[:], scalar=rsE[:, c:c + 1], in1=fg_bc[:],
                                                            op0=ALU.mult, op1=ALU.mult),
                 reads=[xkey, ("rsE", c), "fg_bc"], writes=[xkey])
            T.dma("sync", lambda e: e.dma_start(out=y_d[r:r + 128, :], in_=xbuf[:]),
                  reads=[xkey], writes=[("ydram", xkey[1])], semkey=("xst", xkey[1]))
            XB.unpin(xkey)

        a_T(0)
        for i in range(ntiles + 1):
            cur_ok = i < ntiles
            prev_ok = i >= 1
            pend = None
            if cur_ok:
                if tile_first[i] and tile_seq[i] > 0:
                    load_gate(tile_seq[i])
                b1(i)
                if i == 0:
                    build_conv_w()
                    for blk in (3, 1, 6):
                        load_w_block("in", blk)
                if i + 1 < ntiles:
                    a_load(i + 1)
                if i == 0:
                    load_w_block("out", 0)
                    load_w_block("out", 1)
            if prev_ok:
                us_dma(i - 1)
                d_loads(i - 1)
                conv_a(i - 1, 0)
                conv_a(i - 1, 1)
                c0 = (0,) + conv_b(i - 1, 0)
                c1 = (1,) + conv_b(i - 1, 1)
                conv_a(i - 1, 2)
                conv_a(i - 1, 3)
                ln_pair(i - 1, [c0, c1])
                c2 = (2,) + conv_b(i - 1, 2)
            if cur_ok:
                b2_a(0)
            if prev_ok:
                c3 = (3,) + conv_b(i - 1, 3)
            if cur_ok:
                b2_a(1)
            if prev_ok:
                lb = ln_p1(i - 1, [c2, c3])
            if cur_ok and i + 1 < ntiles:
                a_compute(i + 1)
            if cur_ok:
                b2_a(2)
            if prev_ok:
                ln_p2(i - 1, lb)
            if cur_ok:
                b2_a(3)
            if prev_ok:
                ln_p3(i - 1, lb)
            if cur_ok:
                b2_b(i, 0)
            if prev_ok:
                ln_p4(i - 1, lb)
            if cur_ok:
                for j in range(1, 4):
                    b2_b(i, j)
            if i == 0:
                gate_part()
            nxt = i + 1 if i + 1 < ntiles else None
            if prev_ok:
                stage_d(i - 1, nxt)
            elif nxt is not None:
                a_T(nxt)
        T._need("sync", [(k, v) for k, v in T.count.items() if isinstance(k, tuple) and k[0] == "xst"])
        T.emit()
    return nc


def _core_inputs(k, x_prompt, x_sample, c_prompt, c_sample, shared):
    xs4 = x_sample[4 * k:4 * k + 4].reshape(-1, D)
    x = np.concatenate([x_prompt[k], xs4], axis=0)
    cT = np.ascontiguousarray(np.concatenate([c_prompt[k][None], c_sample[4 * k:4 * k + 4]], axis=0).T)
    m = {"x": np.ascontiguousarray(x, dtype=np.float32), "cT": cT.astype(np.float32)}
    m.update(shared)
    return m


def _wsel(w):
    out = np.zeros((128, 4, 2, 16), np.float32)
    p = np.arange(128)
    for j in range(4):
        for h in range(2):
            nat = (p // 64) == h
            for q in range(16):
                out[nat, j, h, q] = w[2 * q, j * 128 + p[nat]]
                if 2 * q + 1 < KB:
                    out[~nat, j, h, q] = w[2 * q + 1, j * 128 + (p[~nat] + 64) % 128]
    return np.ascontiguousarray(out.reshape(128, 128))


def _shared(norm_g, w_ada, b_ada, w_in, conv_a_w, conv_b_w, conv_b_b, ln_g, ln_b, w_out, final_g):
    f = lambda a: np.ascontiguousarray(np.asarray(a, dtype=np.float32))
    return {
        "ngT": f(np.asarray(norm_g)[0].reshape(8, 128).T),
        "w_ada": f(np.asarray(w_ada)[0]),
        "b_ada": f(np.asarray(b_ada)[0].reshape(1, 3 * D)),
        "w_in": f(np.asarray(w_in)[0]),
        "caT": f(np.asarray(conv_a_w)[0].reshape(KA, 4, 128).transpose(2, 1, 0).reshape(128, 4 * KA)),
        "wsel": _wsel(np.asarray(conv_b_w, dtype=np.float32)[0]),
        "cbbT": f(np.asarray(conv_b_b)[0].reshape(4, 128).T),
        "lngT": f(np.asarray(ln_g)[0].reshape(4, 128).T),
        "lnbT": f(np.asarray(ln_b)[0].reshape(4, 128).T),
        "w_out": f(np.asarray(w_out)[0]),
        "fg": f(np.asarray(final_g).reshape(1, D)),
    }


def kernel(x_prompt, x_sample, c_prompt, c_sample, norm_g, w_ada, b_ada, w_in,
           conv_a_w, conv_b_w, conv_b_b, ln_g, ln_b, w_out, final_g):
    x_prompt = np.asarray(x_prompt, dtype=np.float32)
    x_sample = np.asarray(x_sample, dtype=np.float32)
    c_prompt = np.asarray(c_prompt, dtype=np.float32)
    c_sample = np.asarray(c_sample, dtype=np.float32)
    shared = _shared(norm_g, w_ada, b_ada, w_in, conv_a_w, conv_b_w, conv_b_b, ln_g, ln_b, w_out, final_g)
    nc = build_nc(SEQ_TILES)
    in_maps = [_core_inputs(k, x_prompt, x_sample, c_prompt, c_sample, shared) for k in range(N_CORES)]
    res = run_bass_kernel_spmd(nc, in_maps, core_ids=list(range(N_CORES)))
    y_prompt = np.empty_like(x_prompt)
    y_sample = np.empty_like(x_sample)
    for k in range(N_CORES):
        y = res.results[k]["y"]
        y_prompt[k] = y[:8192]
        y_sample[4 * k:4 * k + 4] = y[8192:].reshape(4, 2048, D)
    return (y_prompt, y_sample)
```

```python
import contextlib
import numpy as np
import concourse.bass as bass
import concourse.mybir as mybir
from concourse.bass_utils import run_bass_kernel_spmd

F32 = mybir.dt.float32
BF16 = mybir.dt.bfloat16
AF = mybir.ActivationFunctionType
ALU = mybir.AluOpType

D = 1024
NT = 512
WA_ = 512
INC = 3584
KB = 31
KA = 3
RMS_EPS = 1e-6
LN_EPS = 1e-5
N_CORES = 8
SEQ_TILES = (16, 4, 4, 4, 4)


class Tracker:
    ENGINES = ("tensor", "vector", "scalar", "gpsimd", "sync")

    def __init__(self, nc, stack):
        self.nc = nc
        self.stack = stack
        self.sems = {}
        self.count = {}
        self.streams = {e: [] for e in self.ENGINES}
        self.waited = {e: {} for e in self.ENGINES}
        self.lastw = {}
        self.readers = {}
        self.pending = {e: [] for e in self.ENGINES}
        self.nsem = 0
        self.clock = 0
        self.touch = {}
        for e in ("tensor", "vector", "scalar", "gpsimd"):
            self._sem(e)

    def _sem(self, key):
        if key not in self.sems:
            self.nsem += 1
            self.sems[key] = self.stack.enter_context(self.nc.semaphore("sm%d" % self.nsem))
            self.count[key] = 0
        return self.sems[key]

    def _need(self, eng, evs):
        need = {}
        for ev in evs:
            if ev is None:
                continue
            k, v = ev
            if v > need.get(k, 0):
                need[k] = v
        for k, v in need.items():
            if self.waited[eng].get(k, 0) >= v:
                continue
            self.waited[eng][k] = v
            sem = self.sems[k]
            self.streams[eng].append(lambda e, sem=sem, v=v: e.wait_ge(sem, v))

    def _deps(self, eng, reads, writes):
        evs = []
        for b in reads:
            evs.append(self.lastw.get(b))
        for b in writes:
            evs.append(self.lastw.get(b))
            evs.extend(self.readers.get(b, ()))
        return evs

    def _touch(self, reads, writes):
        self.clock += 1
        for b in reads:
            self.touch[b] = self.clock
        for b in writes:
            self.touch[b] = self.clock

    def op(self, eng, fn, reads=(), writes=(), signal=True):
        self._touch(reads, writes)
        self._need(eng, self._deps(eng, reads, writes))
        self.pending[eng].append((tuple(reads), tuple(writes)))
        if signal:
            self.count[eng] += 1
            v = self.count[eng]
            sem = self.sems[eng]
            self.streams[eng].append(lambda e, fn=fn, sem=sem: fn(e).then_inc(sem, 1))
            ev = (eng, v)
            for rs, ws in self.pending[eng]:
                for b in rs:
                    self.readers.setdefault(b, []).append(ev)
                for b in ws:
                    self.lastw[b] = ev
                    self.readers[b] = []
            self.pending[eng] = []
        else:
            self.streams[eng].append(lambda e, fn=fn: fn(e))

    def dma(self, eng, fn, reads=(), writes=(), semkey=None):
        self._touch(reads, writes)
        self._sem(semkey)
        self._need(eng, self._deps(eng, reads, writes))
        self.count[semkey] += 16
        v = self.count[semkey]
        sem = self.sems[semkey]
        self.streams[eng].append(lambda e, fn=fn, sem=sem: fn(e).then_inc(sem, 16))
        ev = (semkey, v)
        for b in reads:
            self.readers.setdefault(b, []).append(ev)
        for b in writes:
            self.lastw[b] = ev
            self.readers[b] = []
        return ev

    def emit(self):
        with self.nc.Block() as block:
            for name in self.ENGINES:
                stream = self.streams[name]
                if not stream:
                    continue

                def body(e, stream=stream):
                    for f in stream:
                        f(e)

                getattr(block, name)(body)


class Ring:
    def __init__(self, name, bufs, tracker):
        self.name, self.bufs, self.t = name, bufs, tracker
        self.pinned = set()

    def next(self, pin=False):
        cands = [j for j in range(len(self.bufs)) if j not in self.pinned]
        j = min(cands, key=lambda j: self.t.touch.get((self.name, j), -1 + 0.001 * j))
        self.t.clock += 1
        self.t.touch[(self.name, j)] = self.t.clock
        if pin:
            self.pinned.add(j)
        return self.bufs[j], (self.name, j)

    def unpin(self, key):
        self.pinned.discard(key[1])


def build_nc(seq_tiles=SEQ_TILES):
    nseq = len(seq_tiles)
    ntiles = sum(seq_tiles)
    tile_seq, tile_first, tile_last = [], [], []
    for s, n in enumerate(seq_tiles):
        for t in range(n):
            tile_seq.append(s)
            tile_first.append(t == 0)
            tile_last.append(t == n - 1)

    nc = bass.Bass("TRN2", target_bir_lowering=False)
    x_d = nc.dram_tensor("x", [ntiles * NT, D], F32, kind="ExternalInput").ap()
    cT_d = nc.dram_tensor("cT", [D, nseq], F32, kind="ExternalInput").ap()
    ng_d = nc.dram_tensor("ngT", [128, 8], F32, kind="ExternalInput").ap()
    wada_d = nc.dram_tensor("w_ada", [D, 3 * D], F32, kind="ExternalInput").ap()
    bada_d = nc.dram_tensor("b_ada", [1, 3 * D], F32, kind="ExternalInput").ap()
    win_d = nc.dram_tensor("w_in", [D, INC], F32, kind="ExternalInput").ap()
    ca_d = nc.dram_tensor("caT", [128, 4 * KA], F32, kind="ExternalInput").ap()
    wsel_d = nc.dram_tensor("wsel", [128, 4 * 2 * 16], F32, kind="ExternalInput").ap()
    cbb_d = nc.dram_tensor("cbbT", [128, 4], F32, kind="ExternalInput").ap()
    lng_d = nc.dram_tensor("lngT", [128, 4], F32, kind="ExternalInput").ap()
    lnb_d = nc.dram_tensor("lnbT", [128, 4], F32, kind="ExternalInput").ap()
    wout_d = nc.dram_tensor("w_out", [D, D], F32, kind="ExternalInput").ap()
    fg_d = nc.dram_tensor("fg", [1, D], F32, kind="ExternalInput").ap()
    y_d = nc.dram_tensor("y", [ntiles * NT, D], F32, kind="ExternalOutput").ap()
    gate_d = nc.dram_tensor("gate_scratch", [nseq, D], F32).ap()

    with contextlib.ExitStack() as st:
        def sb(name, shape, dt):
            return st.enter_context(nc.sbuf_tensor(name, shape, dt))

        def ps(name, shape, dt):
            return st.enter_context(nc.psum_tensor(name, shape, dt))

        T = Tracker(nc, st)

        w_in_sb = sb("w_in_sb", [128, 8, INC], BF16)
        w_out_sb = sb("w_out_sb", [128, 8, D], BF16)
        WC = sb("WC", [128, 4, 2, 16, 64], BF16)
        Cc2 = sb("Cc2", [128, 64], F32)
        wsel = sb("wsel_sb", [128, 4, 2, 16], F32)
        caH = sb("caH", [128, 4 * KA], F32)
        Bm = sb("Bm", [128, 128], BF16)
        ident = sb("ident", [128, 128], BF16)
        identf = sb("identf", [128, 128], F32)
        Cm = sb("Cm", [128, 128], F32)
        ones = sb("ones", [128, 8], F32)
        mhalf = sb("mhalf", [128, 4], F32)
        cT_sb = sb("cT_sb", [128, 8, nseq], F32)
        thc = sb("thc", [128, 8, nseq], F32)
        scT = sb("scT", [128, 8, nseq], F32)
        ngT = sb("ngT_sb", [128, 8], F32)
        caT = sb("caT_sb", [128, 4 * KA], F32)
        cbb = sb("cbb_sb", [128, 4], F32)
        lng = sb("lng_sb", [128, 4], F32)
        lnb = sb("lnb_sb", [128, 4], F32)
        qg = sb("qg", [128, 4], F32)
        qb = sb("qb", [128, 4], F32)
        cbs = sb("cbs", [128, 4], F32)
        gsT = sb("gsT", [128, 8, nseq], F32)
        shT = sb("shT", [128, 8, nseq], F32)
        fg_bc = sb("fg_bc", [128, D], F32)
        gate_bc = [sb("gate_bc%d" % i, [128, D], F32) for i in range(2)]
        xs = sb("xs", [128, 4, D], BF16)
        gb2 = sb("gb2", [128, 2, NT], BF16)
        dmy = sb("dmy", [128, 2], F32)
        hT = sb("hT", [128, 8, NT], BF16)
        yT = sb("yT", [128, 8, NT], BF16)
        u_sb = [sb("u%d" % i, [128, 4, 2, NT + 30], BF16) for i in range(2)]
        t1_sb = [sb("t1%d" % i, [128, 4, NT + 2], BF16) for i in range(2)]
        ga = sb("ga", [128, 4, NT], BF16)
        gb = sb("gb", [128, 4, NT], BF16)
        sqr = Ring("sq", [sb("sq%d" % i, [128, NT], BF16) for i in range(2)], T)
        ssA = sb("ssA", [128, 4], F32)
        msA = sb("msA", [128, 4], F32)
        rsA = sb("rsA", [128, 4], F32)
        ssE = sb("ssE", [128, 8], F32)
        msE = sb("msE", [128, 8], F32)
        rsE = sb("rsE", [128, 8], F32)
        XB = Ring("xb", [sb("xb%d" % i, [128, D], F32) for i in range(8)], T)
        TP = Ring("tp", [sb("tp%d" % i, [128, NT], F32) for i in range(5)], T)
        PB = Ring("pb", [ps("pb%d" % i, [128, NT], F32) for i in range(6)], T)
        ptr_t = [ps("ptr%d" % i, [128, 2 * NT], BF16) for i in range(2)]
        PTR = Ring("ptr", [ptr_t[0][:, 0:NT], ptr_t[1][:, 0:NT]], T)

        cur = {"eidx": 0}

        def mk_ident(t, key):
            T.op("gpsimd", lambda e: e.memset(t[:], 0.0), writes=[key])
            T.op("gpsimd", lambda e: e.affine_select(out=t[:], in_=t[:], compare_op=ALU.not_equal, fill=1.0,
                                                     base=0, pattern=[[-1, 128]], channel_multiplier=1),
                 reads=[key], writes=[key])

        wcnt = {"n": 0}

        def load_w_block(kind, blk):
            for kc in range(8):
                load_w_piece(kind, blk, kc)

        def load_w_piece(kind, blk, kc):
            if True:
                stg, skey = TP.next()
                if kind == "in":
                    src = win_d[kc * 128:(kc + 1) * 128, blk * NT:(blk + 1) * NT]
                    dst = w_in_sb[:, kc, blk * NT:(blk + 1) * NT]
                    wkey = ("w_in", blk)
                else:
                    src = wout_d[kc * 128:(kc + 1) * 128, blk * NT:(blk + 1) * NT]
                    dst = w_out_sb[:, kc, blk * NT:(blk + 1) * NT]
                    wkey = ("w_out", blk)
                T.dma("sync", lambda e, stg=stg, src=src: e.dma_start(out=stg[:], in_=src), writes=[skey], semkey=("tpl", skey[1]))
                wcnt["n"] += 1
                if wcnt["n"] % 2 == 0:
                    T.op("vector", lambda e, stg=stg, dst=dst: e.tensor_copy(out=dst, in_=stg[:]), reads=[skey], writes=[wkey])
                else:
                    T.op("scalar", lambda e, stg=stg, dst=dst: e.activation(out=dst, in_=stg[:], func=AF.Copy), reads=[skey], writes=[wkey])

        mk_ident(ident, "ident")
        mk_ident(identf, "identf")

        def mk_block(t, key, val, diag=None):
            T.op("gpsimd", lambda e: e.memset(t[:], val), writes=[key])
            T.op("gpsimd", lambda e: e.affine_select(out=t[:, 0:64], in_=t[:, 0:64], compare_op=ALU.is_ge, fill=0.0,
                                                     base=63, pattern=[[0, 64]], channel_multiplier=-1),
                 reads=[key], writes=[key])
            T.op("gpsimd", lambda e: e.affine_select(out=t[:, 64:128], in_=t[:, 64:128], compare_op=ALU.is_ge, fill=0.0,
                                                     base=-64, pattern=[[0, 64]], channel_multiplier=1),
                 reads=[key], writes=[key])
            if diag is not None:
                T.op("gpsimd", lambda e: e.affine_select(out=t[:], in_=t[:], compare_op=ALU.not_equal, fill=diag,
                                                         base=0, pattern=[[-1, 128]], channel_multiplier=1),
                     reads=[key], writes=[key])

        mk_block(Cm, "Cm", -1.0 / 64, diag=63.0 / 64)
        mk_block(Bm, "Bm", 1.0 / 64)
        T.op("gpsimd", lambda e: e.tensor_tensor(out=Cc2[:], in0=Cm[:, 0:64], in1=Cm[:, 64:128], op=ALU.add), reads=["Cm"], writes=["Cc2"])
        for i_ in range(2):
            T.op("gpsimd", lambda e, i_=i_: e.memset(u_sb[i_][:], 0.0), writes=[("u", i_)] + [("us", i_, j_) for j_ in range(4)])
        T.op("gpsimd", lambda e: e.memset(ones[:], 1.0), writes=["ones"])
        T.op("gpsimd", lambda e: e.memset(mhalf[:], -0.5), writes=["mhalf"])

        small = [(cT_sb, cT_d.rearrange("(k p) s -> p k s", p=128), "cT"), (ngT, ng_d, "ngT"), (caT, ca_d, "caT"),
                 (wsel, wsel_d.rearrange("p (j h q) -> p j h q", j=4, h=2), "wsel"), (cbb, cbb_d, "cbb"), (lng, lng_d, "lng"), (lnb, lnb_d, "lnb")]
        for t, d, key in small:
            T.dma("sync", lambda e, t=t, d=d: e.dma_start(out=t[:], in_=d), writes=[key], semkey=("ld", key))
        bstage = [(fg_bc, "fg_bc"), (gate_bc[0], ("gate_bc", 0)), (gate_bc[1], ("gate_bc", 1))]
        for q, (t, key) in enumerate(bstage):
            T.dma("sync", lambda e, t=t, q=q: e.dma_start(out=t[0:1, :], in_=bada_d[0:1, q * D:(q + 1) * D]),
                  writes=[key], semkey=("ldb", q))

        T.op("scalar", lambda e: e.activation(out=thc[:], in_=cT_sb[:], func=AF.Tanh, scale=0.5), reads=["cT"], writes=["thc"])
        T.op("vector", lambda e: e.scalar_tensor_tensor(out=scT[:], in0=thc[:], scalar=1.0, in1=cT_sb[:], op0=ALU.add, op1=ALU.mult),
             reads=["thc", "cT"], writes=["scT"])
        T.op("vector", lambda e: e.tensor_scalar(out=scT[:], in0=scT[:], scalar1=0.5, scalar2=None, op0=ALU.mult),
             reads=["scT"], writes=["scT"])

        def a_load(i):
            row0 = i * NT
            for sub in range(4):
                xbuf, xkey = XB.next(pin=True)
                T.dma("sync", lambda e, xbuf=xbuf, r=row0 + sub * 128: e.dma_start(out=xbuf[:], in_=x_d[r:r + 128, :]),
                      writes=[xkey], semkey=("xl", xkey[1]))
                a_prep.bufs[sub] = (xbuf, xkey)

        def a_compute(i):
            for sub in range(4):
                xbuf, xkey = a_prep.bufs[sub]
                T.op("scalar", lambda e, xbuf=xbuf, sub=sub: e.activation(out=xs[:, sub, :], in_=xbuf[:], func=AF.Square,
                                                                          accum_out=ssA[:, sub:sub + 1]),
                     reads=[xkey], writes=[("xs", sub), "ssA"])
            T.op("vector", lambda e: e.tensor_scalar(out=msA[:], in0=ssA[:], scalar1=1.0 / D, scalar2=RMS_EPS, op0=ALU.mult, op1=ALU.add),
                 reads=["ssA"], writes=["msA"])
            T.op("gpsimd", lambda e: e.tensor_tensor(out=rsA[:], in0=msA[:], in1=mhalf[:], op=ALU.pow), reads=["msA", "mhalf"], writes=["rsA"])
            for sub in range(4):
                xbuf, xkey = a_prep.bufs[sub]
                T.op("gpsimd", lambda e, xbuf=xbuf, sub=sub: e.tensor_scalar(out=xs[:, sub, :], in0=xbuf[:], scalar1=rsA[:, sub:sub + 1],
                                                                             scalar2=1.0, op0=ALU.mult, op1=ALU.mult),
                     reads=[xkey, "rsA"], writes=[("xs", sub)])
                XB.unpin(xkey)

        def a_prep(i):
            a_load(i)
            a_compute(i)
        a_prep.bufs = [None] * 4

        def mod_thirds(thirds, between):
            banks = {}
            for third in thirds:
                banks[third] = [PB.next(pin=True), PB.next(pin=True)]
            for kc in range(8):
                for third in thirds:
                    xbuf, xkey = XB.next()
                    T.dma("sync", lambda e, xbuf=xbuf, kc=kc, third=third: e.dma_start(
                        out=xbuf[:], in_=wada_d[kc * 128:(kc + 1) * 128, third * D:(third + 1) * D]),
                        writes=[xkey], semkey=("xl", xkey[1]))
                    for h in range(2):
                        bank, bkey = banks[third][h]
                        T.op("tensor", lambda e, bank=bank, xbuf=xbuf, kc=kc, h=h: e.matmul(
                            bank[0:nseq, :], lhsT=scT[:, kc, :], rhs=xbuf[:, h * NT:(h + 1) * NT], start=(kc == 0), stop=False),
                            reads=["scT", xkey], writes=[bkey], signal=(h == 1))
                    between()
            out = {}
            for third in thirds:
                t, key = bstage[third]
                for h in range(2):
                    bank, bkey = banks[third][h]
                    T.op("tensor", lambda e, bank=bank, t=t, h=h: e.matmul(
                        bank[0:nseq, :], lhsT=ones[0:1, 0:nseq], rhs=t[0:1, h * NT:(h + 1) * NT], start=False, stop=True),
                        reads=["ones", key], writes=[bkey])
            for third in thirds:
                mbuf, mkey = XB.next()
                for h in range(2):
                    bank, bkey = banks[third][h]
                    T.op("vector", lambda e, mbuf=mbuf, bank=bank, h=h: e.tensor_copy(
                        out=mbuf[0:nseq, h * NT:(h + 1) * NT], in_=bank[0:nseq, :]), reads=[bkey], writes=[mkey])
                    PB.unpin(bkey)
                out[third] = (mbuf, mkey)
            return out

        first_pieces = [("in", blk, kc) for blk in (5, 4, 0, 2) for kc in range(8)]

        def between():
            for _ in range(2):
                if first_pieces:
                    load_w_piece(*first_pieces.pop(0))

        a_prep(0)
        mb = mod_thirds([0, 1], between)
        while first_pieces:
            load_w_piece(*first_pieces.pop(0))
        modbuf = [mb[0], mb[1], None]
        T.dma("sync", lambda e: e.dma_start(out=fg_bc[:], in_=fg_d.partition_broadcast(128)), writes=["fg_bc"], semkey="ld_fg")

        def gate_part():
            g = mod_thirds([2], lambda: None)[2]
            T.dma("sync", lambda e: e.dma_start(out=gate_d, in_=g[0][0:nseq, :]), reads=[g[1]], writes=["gate_d"], semkey="st_gate")
            load_gate(0)

        bank_sh, bkey_sh = PB.next()
        bank_sc, bkey_sc = PB.next()
        for kc in range(8):
            T.op("tensor", lambda e, kc=kc: e.transpose(out=bank_sh[:, kc * nseq:(kc + 1) * nseq],
                                                        in_=modbuf[0][0][0:nseq, kc * 128:(kc + 1) * 128],
                                                        identity=identf[0:nseq, 0:nseq]),
                 reads=[modbuf[0][1], "identf"], writes=[bkey_sh], signal=(kc == 7))
        for kc in range(8):
            T.op("tensor", lambda e, kc=kc: e.transpose(out=bank_sc[:, kc * nseq:(kc + 1) * nseq],
                                                        in_=modbuf[1][0][0:nseq, kc * 128:(kc + 1) * 128],
                                                        identity=identf[0:nseq, 0:nseq]),
                 reads=[modbuf[1][1], "identf"], writes=[bkey_sc], signal=(kc == 7))
        T.op("vector", lambda e: e.tensor_copy(out=shT[:].rearrange("p k s -> p (k s)"), in_=bank_sh[:, 0:8 * nseq]),
             reads=[bkey_sh], writes=["shT"])
        for kc in range(8):
            T.op("vector", lambda e, kc=kc: e.tensor_scalar(out=gsT[:, kc, :], in0=bank_sc[:, kc * nseq:(kc + 1) * nseq],
                                                            scalar1=1.0, scalar2=ngT[:, kc:kc + 1], op0=ALU.add, op1=ALU.mult),
                 reads=[bkey_sc, "ngT"], writes=["gsT"])

        bank_cb, bkey_cb = PB.next()
        T.op("tensor", lambda e: e.matmul(bank_cb[:, 0:4], lhsT=Cm[:], rhs=cbb[:], start=True, stop=True),
             reads=["Cm", "cbb"], writes=[bkey_cb])
        T.op("vector", lambda e: e.tensor_copy(out=cbs[:], in_=bank_cb[:, 0:4]), reads=[bkey_cb], writes=["cbs"])
        T.op("vector", lambda e: e.tensor_scalar(out=caH[:], in0=caT[:], scalar1=0.5, scalar2=None, op0=ALU.mult), reads=["caT"], writes=["caH"])
        T.op("vector", lambda e: e.tensor_scalar(out=qg[:], in0=lng[:], scalar1=0.25, scalar2=None, op0=ALU.mult), reads=["lng"], writes=["qg"])
        T.op("vector", lambda e: e.tensor_scalar(out=qb[:], in0=lnb[:], scalar1=0.25, scalar2=None, op0=ALU.mult), reads=["lnb"], writes=["qb"])

        def build_conv_w():
            n_w = 0
            for j in range(4):
                for h in range(2):
                    for q in range(16):
                        eng = "vector" if (n_w % 2 == 0) else "gpsimd"
                        n_w += 1
                        T.op(eng, lambda e, j=j, h=h, q=q: e.tensor_scalar(out=WC[:, j, h, q, :], in0=Cc2[:], scalar1=wsel[:, j, h, q:q + 1],
                                                                           scalar2=0.5, op0=ALU.mult, op1=ALU.mult),
                             reads=["Cc2", "wsel"], writes=[("WC", j, h, q)])

        def load_gate(s):
            g = gate_bc[s % 2]
            T.dma("sync", lambda e, g=g, s=s: e.dma_start(out=g[:], in_=gate_d[s:s + 1, :].partition_broadcast(128)),
                  reads=["gate_d"], writes=[("gate_bc", s % 2)], semkey=("ld_gate", s % 2))

        def a_T(i):
            for kc in range(8):
                a_T_group(i, kc)

        def a_T_group(i, kc):
            s = tile_seq[i]
            if True:
                pt, pkey = PTR.next()
                for sub in range(4):
                    T.op("tensor", lambda e, pt=pt, sub=sub, kc=kc: e.transpose(out=pt[:, sub * 128:(sub + 1) * 128],
                                                                                in_=xs[:, sub, kc * 128:(kc + 1) * 128],
                                                                                identity=ident[:]),
                         reads=[("xs", sub), "ident"], writes=[pkey], signal=(sub == 3))
                if kc % 2 == 0:
                    T.op("scalar", lambda e, pt=pt, kc=kc, s=s: e.activation(out=hT[:, kc, :], in_=pt, func=AF.Identity,
                                                                            scale=gsT[:, kc, s:s + 1], bias=shT[:, kc, s:s + 1]),
                         reads=[pkey, "gsT", "shT"], writes=[("hT", kc)])
                else:
                    T.op("vector", lambda e, pt=pt, kc=kc, s=s: e.tensor_scalar(out=hT[:, kc, :], in0=pt, scalar1=gsT[:, kc, s:s + 1],
                                                                               scalar2=shT[:, kc, s:s + 1], op0=ALU.mult, op1=ALU.add),
                         reads=[pkey, "gsT", "shT"], writes=[("hT", kc)])

        hT_keys = [("hT", kc) for kc in range(8)]

        def mm1(col0):
            bank, bkey = PB.next()
            for kc in range(8):
                T.op("tensor", lambda e, bank=bank, kc=kc: e.matmul(bank[:], lhsT=w_in_sb[:, kc, col0:col0 + 128], rhs=hT[:, kc, :],
                                                                    start=(kc == 0), stop=(kc == 7)),
                     reads=[("w_in", col0 // NT), ("hT", kc)], writes=[bkey], signal=(kc == 7))
            return bank, bkey

        def b1(i):
            slot = i % 2
            u, t1 = u_sb[slot], t1_sb[slot]
            for j in range(4):
                bank, bkey = mm1(2560 + j * 128)
                th, tkey = TP.next()
                T.op("scalar", lambda e, th=th, bank=bank: e.activation(out=th[:], in_=bank[:], func=AF.Tanh, scale=0.5),
                     reads=[bkey], writes=[tkey])
                bank2, bkey2 = mm1(2048 + j * 128)
                for h in range(2):
                    T.op("vector", lambda e, th=th, bank2=bank2, j=j, h=h: e.scalar_tensor_tensor(
                        out=u[64 * h:64 * h + 64, j, h, 15:15 + NT], in0=th[64 * h:64 * h + 64, :], scalar=1.0,
                        in1=bank2[64 * h:64 * h + 64, :], op0=ALU.add, op1=ALU.mult),
                        reads=[tkey, bkey2], writes=[("u", slot)])
            for j in range(4):
                bank, bkey = mm1(0 + j * 128)
                ain, akey = TP.next()
                T.op("scalar", lambda e, ain=ain, bank=bank: e.activation(out=ain[:], in_=bank[:], func=AF.Copy), reads=[bkey], writes=[akey])
                bank2, bkey2 = mm1(1024 + j * 128)
                T.op("vector", lambda e, ain=ain, bank2=bank2, j=j: e.tensor_tensor(out=t1[:, j, 1:1 + NT], in0=bank2[:], in1=ain[:], op=ALU.mult),
                     reads=[akey, bkey2], writes=[("t1", slot)])
            up, t1p = u_sb[1 - slot], t1_sb[1 - slot]
            if tile_first[i]:
                T.op("gpsimd", lambda e: e.memset(u[:, :, :, 0:15], 0.0), writes=[("u", slot)])
                T.op("gpsimd", lambda e: e.memset(t1[:, :, 0:1], 0.0), writes=[("t1", slot)])
            else:
                T.op("gpsimd", lambda e: e.tensor_copy(out=u[:, :, :, 0:15], in_=up[:, :, :, NT:NT + 15]), reads=[("u", 1 - slot)], writes=[("u", slot)])
                T.op("gpsimd", lambda e: e.tensor_copy(out=up[:, :, :, NT + 15:NT + 30], in_=u[:, :, :, 15:30]), reads=[("u", slot)], writes=[("u", 1 - slot)])
                T.op("gpsimd", lambda e: e.tensor_copy(out=t1[:, :, 0:1], in_=t1p[:, :, NT:NT + 1]), reads=[("t1", 1 - slot)], writes=[("t1", slot)])
                T.op("gpsimd", lambda e: e.tensor_copy(out=t1p[:, :, NT + 1:NT + 2], in_=t1[:, :, 1:2]), reads=[("t1", slot)], writes=[("t1", 1 - slot)])
            if tile_last[i]:
                T.op("gpsimd", lambda e: e.memset(u[:, :, :, NT + 15:NT + 30], 0.0), writes=[("u", slot)])
                T.op("gpsimd", lambda e: e.memset(t1[:, :, NT + 1:NT + 2], 0.0), writes=[("t1", slot)])

        def b2_a(j):
            bank, bkey = mm1(1536 + j * 128)
            th, tkey = TP.next()
            T.op("scalar", lambda e, th=th, bank=bank: e.activation(out=th[:], in_=bank[:], func=AF.Tanh, scale=0.5), reads=[bkey], writes=[tkey])
            T.op("vector", lambda e, th=th, bank=bank: e.scalar_tensor_tensor(out=th[:], in0=th[:], scalar=1.0, in1=bank[:], op0=ALU.add, op1=ALU.mult),
                 reads=[tkey, bkey], writes=[tkey])
            bank2, bkey2 = mm1(512 + j * 128)
            T.op("vector", lambda e, th=th, bank2=bank2, j=j: e.tensor_tensor(out=ga[:, j, :], in0=bank2[:], in1=th[:], op=ALU.mult),
                 reads=[tkey, bkey2], writes=[("ga", j)])

        def gbuf(i, j):
            if j < 2 or i % 2 == 0:
                return gb[:, j, :], ("gb", j)
            return gb2[:, j - 2, :], ("gb2", j)

        def b2_b(i, j):
            gdst, gkey = gbuf(i, j)
            bank, bkey = mm1(3072 + j * 128)
            th, tkey = TP.next()
            T.op("scalar", lambda e, th=th, bank=bank: e.activation(out=th[:], in_=bank[:], func=AF.Tanh, scale=0.5), reads=[bkey], writes=[tkey])
            T.op("vector", lambda e, th=th, bank=bank: e.scalar_tensor_tensor(out=gdst, in0=th[:], scalar=1.0, in1=bank[:], op0=ALU.add, op1=ALU.mult),
                 reads=[tkey, bkey], writes=[gkey])

        def conv_a(i, j):
            slot = i % 2
            t1 = t1_sb[slot]
            acc, akey = TP.next()
            T.op("vector", lambda e: e.tensor_scalar(out=acc[:], in0=t1[:, j, 0:NT], scalar1=caH[:, j * KA:j * KA + 1], scalar2=None, op0=ALU.mult),
                 reads=[("t1", slot), "caH"], writes=[akey])
            for k in (1, 2):
                T.op("vector", lambda e, k=k: e.scalar_tensor_tensor(out=acc[:], in0=t1[:, j, k:k + NT], scalar=caH[:, j * KA + k:j * KA + k + 1],
                                                                   in1=acc[:], op0=ALU.mult, op1=ALU.add),
                     reads=[("t1", slot), "caH", akey], writes=[akey])
            T.op("vector", lambda e: e.tensor_tensor(out=yT[:, j, :], in0=acc[:], in1=ga[:, j, :], op=ALU.mult),
                 reads=[akey, ("ga", j)], writes=[("yT", j)])

        def us_dma(i):
            slot = i % 2
            u = u_sb[slot]
            for h in range(2):
                a, b = 64 * h, 64 * (1 - h)
                for j in range(4):
                    T.dma("sync", lambda e, a=a, b=b, h=h, j=j: e.dma_start(out=u[b:b + 64, j, h, 0:NT + 29], in_=u[a:a + 64, j, h, 1:NT + 30]),
                          reads=[("u", slot)], writes=[("us", slot, j)], semkey=("usd", slot, h, j))

        def conv_b(i, j):
            slot = i % 2
            u = u_sb[slot]
            bank, bkey = PB.next()
            for q in range(16):
                for h in range(2):
                    T.op("tensor", lambda e, bank=bank, q=q, h=h: e.matmul(
                        bank[64 * h:64 * h + 64, :], lhsT=WC[:, j, h, q, :], rhs=u[:, j, h, 2 * q:2 * q + NT],
                        start=(q == 0), stop=(q == 15), tile_position=(0, 64 * h)),
                        reads=[("WC", j, h, q), ("u", slot), ("us", slot, j)], writes=[bkey], signal=(q == 15 and h == 1))
            sq, skey = sqr.next()
            T.op("scalar", lambda e, sq=sq, bank=bank: e.activation(out=sq[:], in_=bank[:], func=AF.Square, bias=cbs[:, j:j + 1], scale=1.0),
                 reads=[bkey, "cbs"], writes=[skey])
            cc, ckey = TP.next(pin=True)
            T.op("scalar", lambda e, cc=cc, bank=bank: e.activation(out=cc[:], in_=bank[:], func=AF.Identity, bias=cbs[:, j:j + 1], scale=1.0),
                 reads=[bkey, "cbs"], writes=[ckey])
            return cc, ckey, sq, skey

        def ln_pair(ti, items):
            st_ = []
            for (j, cc, ckey, sq, skey) in items:
                vb, vkey = PB.next()
                T.op("tensor", lambda e, vb=vb, sq=sq: e.matmul(vb[:], lhsT=Bm[:], rhs=sq[:], start=True, stop=True), reads=["Bm", skey], writes=[vkey])
                r, rkey = TP.next(pin=True)
                st_.append((j, cc, ckey, vb, vkey, r, rkey))
            T.op("scalar", lambda e: e.activation(out=dmy[:, 0:1], in_=ones[:, 0:1], func=AF.Ln), reads=["ones"], writes=["dmy0"])
            for (j, cc, ckey, vb, vkey, r, rkey) in st_:
                T.op("scalar", lambda e, r=r, vb=vb: e.activation(out=r[:], in_=vb[:], func=AF.Ln, bias=LN_EPS, scale=1.0), reads=[vkey], writes=[rkey])
                T.op("scalar", lambda e, r=r: e.activation(out=r[:], in_=r[:], func=AF.Exp, scale=-0.5), reads=[rkey], writes=[rkey])
            T.op("scalar", lambda e: e.activation(out=dmy[:, 1:2], in_=ones[:, 0:1], func=AF.Tanh), reads=["ones"], writes=["dmy1"])
            for (j, cc, ckey, vb, vkey, r, rkey) in st_:
                T.op("vector", lambda e, r=r, cc=cc: e.tensor_tensor(out=r[:], in0=cc[:], in1=r[:], op=ALU.mult),
                     reads=[ckey, rkey], writes=[rkey])
                T.op("vector", lambda e, r=r, j=j: e.tensor_scalar(out=r[:], in0=r[:], scalar1=qg[:, j:j + 1], scalar2=qb[:, j:j + 1], op0=ALU.mult, op1=ALU.add),
                     reads=[rkey, "qg", "qb"], writes=[rkey])
            for (j, cc, ckey, vb, vkey, r, rkey) in st_:
                T.op("scalar", lambda e, cc=cc, r=r: e.activation(out=cc[:], in_=r[:], func=AF.Tanh, scale=2.0), reads=[rkey], writes=[ckey])
            for (j, cc, ckey, vb, vkey, r, rkey) in st_:
                T.op("vector", lambda e, cc=cc, r=r: e.scalar_tensor_tensor(out=cc[:], in0=cc[:], scalar=1.0, in1=r[:], op0=ALU.add, op1=ALU.mult),
                     reads=[ckey, rkey], writes=[ckey])
                TP.unpin(rkey)
                gsrc, gkey = gbuf(ti, j)
                T.op("gpsimd", lambda e, cc=cc, j=j, gsrc=gsrc: e.tensor_tensor(out=yT[:, 4 + j, :], in0=cc[:], in1=gsrc, op=ALU.mult),
                     reads=[ckey, gkey], writes=[("yT", 4 + j)])
                TP.unpin(ckey)

        def d_loads(i):
            row0 = i * NT
            for sub in range(4):
                xbuf, xkey = XB.next(pin=True)
                T.dma("sync", lambda e, xbuf=xbuf, r=row0 + sub * 128: e.dma_start(out=xbuf[:], in_=x_d[r:r + 128, :]),
                      writes=[xkey], semkey=("xl", xkey[1]))
                d_loads.bufs[sub] = (xbuf, xkey)
        d_loads.bufs = [None] * 4

        def stage_d(i, nxt=None):
            s = tile_seq[i]
            g = gate_bc[s % 2]
            gkey = ("gate_bc", s % 2)
            row0 = i * NT
            fin = []
            for sub in range(4):
                xbuf, xkey = d_loads.bufs[sub]
                for h in range(2):
                    if nxt is not None and sub < 2:
                        a_T_group(nxt, (sub * 2 + h) * 2)
                        a_T_group(nxt, (sub * 2 + h) * 2 + 1)
                    bank, bkey = PB.next()
                    for kc in range(8):
                        T.op("tensor", lambda e, bank=bank, kc=kc, sub=sub, h=h: e.matmul(
                            bank[:], lhsT=yT[:, kc, sub * 128:(sub + 1) * 128], rhs=w_out_sb[:, kc, h * NT:(h + 1) * NT],
                            start=(kc == 0), stop=(kc == 7)),
                            reads=[("yT", kc), ("w_out", h)], writes=[bkey], signal=(kc == 7))
                    tmp, tkey = TP.next()
                    T.op("vector", lambda e, tmp=tmp, bank=bank, h=h: e.tensor_tensor(out=tmp[:], in0=bank[:], in1=g[:, h * NT:(h + 1) * NT], op=ALU.mult),
                         reads=[bkey, gkey], writes=[tkey])
                    T.op("gpsimd", lambda e, tmp=tmp, xbuf=xbuf, h=h: e.tensor_tensor(out=xbuf[:, h * NT:(h + 1) * NT], in0=tmp[:],
                                                                                      in1=xbuf[:, h * NT:(h + 1) * NT], op=ALU.add),
                         reads=[tkey, xkey], writes=[xkey])
                fin.append((xbuf, xkey, row0 + sub * 128))
                if len(fin) > 1:
                    d_finish(*fin.pop(0))
            while fin:
                d_finish(*fin.pop(0))

        def d_finish(xbuf, xkey, r):
            c = cur["eidx"] % 8
            cur["eidx"] += 1
            jb, jkey = TP.next()
            T.op("scalar", lambda e: e.activation(out=jb[:].bitcast(BF16), in_=xbuf[:], func=AF.Square, accum_out=ssE[:, c:c + 1]),
                 reads=[xkey], writes=[jkey, ("ssE", c)])
            T.op("vector", lambda e: e.tensor_scalar(out=msE[:, c:c + 1], in0=ssE[:, c:c + 1], scalar1=1.0 / D, scalar2=RMS_EPS,
                                                     op0=ALU.mult, op1=ALU.add), reads=[("ssE", c)], writes=[("msE", c)])
            T.op("gpsimd", lambda e: e.tensor_tensor(out=rsE[:, c:c + 1], in0=msE[:, c:c + 1], in1=mhalf[:, 0:1], op=ALU.pow),
                 reads=[("msE", c), "mhalf"], writes=[("rsE", c)])
            T.op("vector", lambda e: e.scalar_tensor_tensor(out=xbuf[:], in0=xbuf[:], scalar=rsE[:, c:c + 1], in1=fg_bc[:],
                                                            op0=ALU.mult, op1=ALU.mult),
                 reads=[xkey, ("rsE", c), "fg_bc"], writes=[xkey])
            T.dma("sync", lambda e: e.dma_start(out=y_d[r:r + 128, :], in_=xbuf[:]),
                  reads=[xkey], writes=[("ydram", xkey[1])], semkey=("xst", xkey[1]))
            XB.unpin(xkey)

        a_T(0)
        for i in range(ntiles + 1):
            cur_ok = i < ntiles
            prev_ok = i >= 1
            pend = None
            if cur_ok:
                if tile_first[i] and tile_seq[i] > 0:
                    load_gate(tile_seq[i])
                b1(i)
                if i == 0:
                    build_conv_w()
                    for blk in (3, 1, 6):
                        load_w_block("in", blk)
                if i + 1 < ntiles:
                    a_load(i + 1)
                if i == 0:
                    load_w_block("out", 0)
                    load_w_block("out", 1)
            if prev_ok:
                us_dma(i - 1)
                d_loads(i - 1)
                conv_a(i - 1, 0)
                conv_a(i - 1, 1)
                c0 = (0,) + conv_b(i - 1, 0)
                c1 = (1,) + conv_b(i - 1, 1)
                conv_a(i - 1, 2)
                conv_a(i - 1, 3)
                ln_pair(i - 1, [c0, c1])
                c2 = (2,) + conv_b(i - 1, 2)
            if cur_ok:
                b2_a(0)
            if prev_ok:
                c3 = (3,) + conv_b(i - 1, 3)
            if cur_ok:
                b2_a(1)
            if prev_ok:
                ln_pair(i - 1, [c2, c3])
            if cur_ok and i + 1 < ntiles:
                a_compute(i + 1)
            if cur_ok:
                b2_a(2)
                b2_a(3)
                for j in range(4):
                    b2_b(i, j)
            if i == 0:
                gate_part()
            nxt = i + 1 if i + 1 < ntiles else None
            if prev_ok:
                stage_d(i - 1, nxt)
            elif nxt is not None:
                a_T(nxt)
        T._need("sync", [(k, v) for k, v in T.count.items() if isinstance(k, tuple) and k[0] == "xst"])
        T.emit()
    return nc


def _core_inputs(k, x_prompt, x_sample, c_prompt, c_sample, shared):
    xs4 = x_sample[4 * k:4 * k + 4].reshape(-1, D)
    x = np.concatenate([x_prompt[k], xs4], axis=0)
    cT = np.ascontiguousarray(np.concatenate([c_prompt[k][None], c_sample[4 * k:4 * k + 4]], axis=0).T)
    m = {"x": np.ascontiguousarray(x, dtype=np.float32), "cT": cT.astype(np.float32)}
    m.update(shared)
    return m


def _wsel(w):
    out = np.zeros((128, 4, 2, 16), np.float32)
    p = np.arange(128)
    for j in range(4):
        for h in range(2):
            nat = (p // 64) == h
            for q in range(16):
                out[nat, j, h, q] = w[2 * q, j * 128 + p[nat]]
                if 2 * q + 1 < KB:
                    out[~nat, j, h, q] = w[2 * q + 1, j * 128 + (p[~nat] + 64) % 128]
    return np.ascontiguousarray(out.reshape(128, 128))


def _shared(norm_g, w_ada, b_ada, w_in, conv_a_w, conv_b_w, conv_b_b, ln_g, ln_b, w_out, final_g):
    f = lambda a: np.ascontiguousarray(np.asarray(a, dtype=np.float32))
    return {
        "ngT": f(np.asarray(norm_g)[0].reshape(8, 128).T),
        "w_ada": f(np.asarray(w_ada)[0]),
        "b_ada": f(np.asarray(b_ada)[0].reshape(1, 3 * D)),
        "w_in": f(np.asarray(w_in)[0]),
        "caT": f(np.asarray(conv_a_w)[0].reshape(KA, 4, 128).transpose(2, 1, 0).reshape(128, 4 * KA)),
        "wsel": _wsel(np.asarray(conv_b_w, dtype=np.float32)[0]),
        "cbbT": f(np.asarray(conv_b_b)[0].reshape(4, 128).T),
        "lngT": f(np.asarray(ln_g)[0].reshape(4, 128).T),
        "lnbT": f(np.asarray(ln_b)[0].reshape(4, 128).T),
        "w_out": f(np.asarray(w_out)[0]),
        "fg": f(np.asarray(final_g).reshape(1, D)),
    }


def kernel(x_prompt, x_sample, c_prompt, c_sample, norm_g, w_ada, b_ada, w_in,
           conv_a_w, conv_b_w, conv_b_b, ln_g, ln_b, w_out, final_g):
    x_prompt = np.asarray(x_prompt, dtype=np.float32)
    x_sample = np.asarray(x_sample, dtype=np.float32)
    c_prompt = np.asarray(c_prompt, dtype=np.float32)
    c_sample = np.asarray(c_sample, dtype=np.float32)
    shared = _shared(norm_g, w_ada, b_ada, w_in, conv_a_w, conv_b_w, conv_b_b, ln_g, ln_b, w_out, final_g)
    nc = build_nc(SEQ_TILES)
    in_maps = [_core_inputs(k, x_prompt, x_sample, c_prompt, c_sample, shared) for k in range(N_CORES)]
    res = run_bass_kernel_spmd(nc, in_maps, core_ids=list(range(N_CORES)))
    y_prompt = np.empty_like(x_prompt)
    y_sample = np.empty_like(x_sample)
    for k in range(N_CORES):
        y = res.results[k]["y"]
        y_prompt[k] = y[:8192]
        y_sample[4 * k:4 * k + 4] = y[8192:].reshape(4, 2048, D)
    return (y_prompt, y_sample)
```

```python
import contextlib
import numpy as np
import concourse.bass as bass
import concourse.mybir as mybir
from concourse.bass_utils import run_bass_kernel_spmd

F32 = mybir.dt.float32
BF16 = mybir.dt.bfloat16
AF = mybir.ActivationFunctionType
ALU = mybir.AluOpType

D = 1024
NT = 512
WA_ = 512
INC = 3584
KB = 31
KA = 3
RMS_EPS = 1e-6
LN_EPS = 1e-5
N_CORES = 8
SEQ_TILES = (16, 4, 4, 4, 4)


class Tracker:
    ENGINES = ("tensor", "vector", "scalar", "gpsimd", "sync")

    def __init__(self, nc, stack):
        self.nc = nc
        self.stack = stack
        self.sems = {}
        self.count = {}
        self.streams = {e: [] for e in self.ENGINES}
        self.waited = {e: {} for e in self.ENGINES}
        self.lastw = {}
        self.readers = {}
        self.pending = {e: [] for e in self.ENGINES}
        self.nsem = 0
        self.clock = 0
        self.touch = {}
        for e in ("tensor", "vector", "scalar", "gpsimd"):
            self._sem(e)

    def _sem(self, key):
        if key not in self.sems:
            self.nsem += 1
            self.sems[key] = self.stack.enter_context(self.nc.semaphore("sm%d" % self.nsem))
            self.count[key] = 0
        return self.sems[key]

    def _need(self, eng, evs):
        need = {}
        for ev in evs:
            if ev is None:
                continue
            k, v = ev
            if v > need.get(k, 0):
                need[k] = v
        for k, v in need.items():
            if self.waited[eng].get(k, 0) >= v:
                continue
            self.waited[eng][k] = v
            sem = self.sems[k]
            self.streams[eng].append(lambda e, sem=sem, v=v: e.wait_ge(sem, v))

    def _deps(self, eng, reads, writes):
        evs = []
        for b in reads:
            evs.append(self.lastw.get(b))
        for b in writes:
            evs.append(self.lastw.get(b))
            evs.extend(self.readers.get(b, ()))
        return evs

    def _touch(self, reads, writes):
        self.clock += 1
        for b in reads:
            self.touch[b] = self.clock
        for b in writes:
            self.touch[b] = self.clock

    def op(self, eng, fn, reads=(), writes=(), signal=True):
        self._touch(reads, writes)
        self._need(eng, self._deps(eng, reads, writes))
        self.pending[eng].append((tuple(reads), tuple(writes)))
        if signal:
            self.count[eng] += 1
            v = self.count[eng]
            sem = self.sems[eng]
            self.streams[eng].append(lambda e, fn=fn, sem=sem: fn(e).then_inc(sem, 1))
            ev = (eng, v)
            for rs, ws in self.pending[eng]:
                for b in rs:
                    self.readers.setdefault(b, []).append(ev)
                for b in ws:
                    self.lastw[b] = ev
                    self.readers[b] = []
            self.pending[eng] = []
        else:
            self.streams[eng].append(lambda e, fn=fn: fn(e))

    def dma(self, eng, fn, reads=(), writes=(), semkey=None):
        self._touch(reads, writes)
        self._sem(semkey)
        self._need(eng, self._deps(eng, reads, writes))
        self.count[semkey] += 16
        v = self.count[semkey]
        sem = self.sems[semkey]
        self.streams[eng].append(lambda e, fn=fn, sem=sem: fn(e).then_inc(sem, 16))
        ev = (semkey, v)
        for b in reads:
            self.readers.setdefault(b, []).append(ev)
        for b in writes:
            self.lastw[b] = ev
            self.readers[b] = []
        return ev

    def emit(self):
        with self.nc.Block() as block:
            for name in self.ENGINES:
                stream = self.streams[name]
                if not stream:
                    continue

                def body(e, stream=stream):
                    for f in stream:
                        f(e)

                getattr(block, name)(body)


class Ring:
    def __init__(self, name, bufs, tracker):
        self.name, self.bufs, self.t = name, bufs, tracker
        self.pinned = set()

    def next(self, pin=False):
        cands = [j for j in range(len(self.bufs)) if j not in self.pinned]
        j = min(cands, key=lambda j: self.t.touch.get((self.name, j), -1 + 0.001 * j))
        self.t.clock += 1
        self.t.touch[(self.name, j)] = self.t.clock
        if pin:
            self.pinned.add(j)
        return self.bufs[j], (self.name, j)

    def unpin(self, key):
        self.pinned.discard(key[1])


def build_nc(seq_tiles=SEQ_TILES):
    nseq = len(seq_tiles)
    ntiles = sum(seq_tiles)
    tile_seq, tile_first, tile_last = [], [], []
    for s, n in enumerate(seq_tiles):
        for t in range(n):
            tile_seq.append(s)
            tile_first.append(t == 0)
            tile_last.append(t == n - 1)

    nc = bass.Bass("TRN2", target_bir_lowering=False)
    x_d = nc.dram_tensor("x", [ntiles * NT, D], F32, kind="ExternalInput").ap()
    cT_d = nc.dram_tensor("cT", [D, nseq], F32, kind="ExternalInput").ap()
    ng_d = nc.dram_tensor("ngT", [128, 8], F32, kind="ExternalInput").ap()
    wada_d = nc.dram_tensor("w_ada", [D, 3 * D], F32, kind="ExternalInput").ap()
    bada_d = nc.dram_tensor("b_ada", [1, 3 * D], F32, kind="ExternalInput").ap()
    win_d = nc.dram_tensor("w_in", [D, INC], F32, kind="ExternalInput").ap()
    ca_d = nc.dram_tensor("caT", [128, 4 * KA], F32, kind="ExternalInput").ap()
    wsel_d = nc.dram_tensor("wsel", [128, 4 * 2 * 16], F32, kind="ExternalInput").ap()
    cbb_d = nc.dram_tensor("cbbT", [128, 4], F32, kind="ExternalInput").ap()
    lng_d = nc.dram_tensor("lngT", [128, 4], F32, kind="ExternalInput").ap()
    lnb_d = nc.dram_tensor("lnbT", [128, 4], F32, kind="ExternalInput").ap()
    wout_d = nc.dram_tensor("w_out", [D, D], F32, kind="ExternalInput").ap()
    fg_d = nc.dram_tensor("fg", [1, D], F32, kind="ExternalInput").ap()
    y_d = nc.dram_tensor("y", [ntiles * NT, D], F32, kind="ExternalOutput").ap()
    gate_d = nc.dram_tensor("gate_scratch", [nseq, D], F32).ap()

    with contextlib.ExitStack() as st:
        def sb(name, shape, dt):
            return st.enter_context(nc.sbuf_tensor(name, shape, dt))

        def ps(name, shape, dt):
            return st.enter_context(nc.psum_tensor(name, shape, dt))

        T = Tracker(nc, st)

        w_in_sb = sb("w_in_sb", [128, 8, INC], BF16)
        w_out_sb = sb("w_out_sb", [128, 8, D], BF16)
        WC = sb("WC", [128, 4, 2, 16, 64], BF16)
        Cc2 = sb("Cc2", [128, 64], F32)
        wsel = sb("wsel_sb", [128, 4, 2, 16], F32)
        caH = sb("caH", [128, 4 * KA], F32)
        Bm = sb("Bm", [128, 128], BF16)
        ident = sb("ident", [128, 128], BF16)
        identf = sb("identf", [128, 128], F32)
        Cm = sb("Cm", [128, 128], F32)
        ones = sb("ones", [128, 8], F32)
        mhalf = sb("mhalf", [128, 4], F32)
        cT_sb = sb("cT_sb", [128, 8, nseq], F32)
        thc = sb("thc", [128, 8, nseq], F32)
        scT = sb("scT", [128, 8, nseq], F32)
        ngT = sb("ngT_sb", [128, 8], F32)
        caT = sb("caT_sb", [128, 4 * KA], F32)
        cbb = sb("cbb_sb", [128, 4], F32)
        lng = sb("lng_sb", [128, 4], F32)
        lnb = sb("lnb_sb", [128, 4], F32)
        qg = sb("qg", [128, 4], F32)
        qb = sb("qb", [128, 4], F32)
        cbs = sb("cbs", [128, 4], F32)
        gsT = sb("gsT", [128, 8, nseq], F32)
        shT = sb("shT", [128, 8, nseq], F32)
        fg_bc = sb("fg_bc", [128, D], F32)
        gate_bc = [sb("gate_bc%d" % i, [128, D], F32) for i in range(2)]
        xs = sb("xs", [128, 4, D], BF16)
        gb2 = sb("gb2", [128, 2, NT], BF16)
        dmy = sb("dmy", [128, 2], F32)
        hT = sb("hT", [128, 8, NT], BF16)
        yT = sb("yT", [128, 8, NT], BF16)
        u_sb = [sb("u%d" % i, [128, 4, 2, NT + 30], BF16) for i in range(2)]
        t1_sb = [sb("t1%d" % i, [128, 4, NT + 2], BF16) for i in range(2)]
        ga = sb("ga", [128, 4, NT], BF16)
        gb = sb("gb", [128, 4, NT], BF16)
        sqr = Ring("sq", [sb("sq%d" % i, [128, NT], BF16) for i in range(2)], T)
        ssA = sb("ssA", [128, 4], F32)
        msA = sb("msA", [128, 4], F32)
        rsA = sb("rsA", [128, 4], F32)
        ssE = sb("ssE", [128, 8], F32)
        msE = sb("msE", [128, 8], F32)
        rsE = sb("rsE", [128, 8], F32)
        XB = Ring("xb", [sb("xb%d" % i, [128, D], F32) for i in range(8)], T)
        TP = Ring("tp", [sb("tp%d" % i, [128, NT], F32) for i in range(5)], T)
        PB = Ring("pb", [ps("pb%d" % i, [128, NT], F32) for i in range(6)], T)
        ptr_t = [ps("ptr%d" % i, [128, 2 * NT], BF16) for i in range(2)]
        PTR = Ring("ptr", [ptr_t[0][:, 0:NT], ptr_t[1][:, 0:NT]], T)

        cur = {"eidx": 0}

        def mk_ident(t, key):
            T.op("gpsimd", lambda e: e.memset(t[:], 0.0), writes=[key])
            T.op("gpsimd", lambda e: e.affine_select(out=t[:], in_=t[:], compare_op=ALU.not_equal, fill=1.0,
                                                     base=0, pattern=[[-1, 128]], channel_multiplier=1),
                 reads=[key], writes=[key])

        wcnt = {"n": 0}

        def load_w_block(kind, blk):
            for kc in range(8):
                load_w_piece(kind, blk, kc)

        def load_w_piece(kind, blk, kc):
            if True:
                stg, skey = TP.next()
                if kind == "in":
                    src = win_d[kc * 128:(kc + 1) * 128, blk * NT:(blk + 1) * NT]
                    dst = w_in_sb[:, kc, blk * NT:(blk + 1) * NT]
                    wkey = ("w_in", blk)
                else:
                    src = wout_d[kc * 128:(kc + 1) * 128, blk * NT:(blk + 1) * NT]
                    dst = w_out_sb[:, kc, blk * NT:(blk + 1) * NT]
                    wkey = ("w_out", blk)
                T.dma("sync", lambda e, stg=stg, src=src: e.dma_start(out=stg[:], in_=src), writes=[skey], semkey=("tpl", skey[1]))
                wcnt["n"] += 1
                if wcnt["n"] % 2 == 0:
                    T.op("vector", lambda e, stg=stg, dst=dst: e.tensor_copy(out=dst, in_=stg[:]), reads=[skey], writes=[wkey])
                else:
                    T.op("scalar", lambda e, stg=stg, dst=dst: e.activation(out=dst, in_=stg[:], func=AF.Copy), reads=[skey], writes=[wkey])

        mk_ident(ident, "ident")
        mk_ident(identf, "identf")

        def mk_block(t, key, val, diag=None):
            T.op("gpsimd", lambda e: e.memset(t[:], val), writes=[key])
            T.op("gpsimd", lambda e: e.affine_select(out=t[:, 0:64], in_=t[:, 0:64], compare_op=ALU.is_ge, fill=0.0,
                                                     base=63, pattern=[[0, 64]], channel_multiplier=-1),
                 reads=[key], writes=[key])
            T.op("gpsimd", lambda e: e.affine_select(out=t[:, 64:128], in_=t[:, 64:128], compare_op=ALU.is_ge, fill=0.0,
                                                     base=-64, pattern=[[0, 64]], channel_multiplier=1),
                 reads=[key], writes=[key])
            if diag is not None:
                T.op("gpsimd", lambda e: e.affine_select(out=t[:], in_=t[:], compare_op=ALU.not_equal, fill=diag,
                                                         base=0, pattern=[[-1, 128]], channel_multiplier=1),
                     reads=[key], writes=[key])

        mk_block(Cm, "Cm", -1.0 / 64, diag=63.0 / 64)
        mk_block(Bm, "Bm", 1.0 / 64)
        T.op("gpsimd", lambda e: e.tensor_tensor(out=Cc2[:], in0=Cm[:, 0:64], in1=Cm[:, 64:128], op=ALU.add), reads=["Cm"], writes=["Cc2"])
        for i_ in range(2):
            T.op("gpsimd", lambda e, i_=i_: e.memset(u_sb[i_][:], 0.0), writes=[("u", i_)] + [("us", i_, j_) for j_ in range(4)])
        T.op("gpsimd", lambda e: e.memset(ones[:], 1.0), writes=["ones"])
        T.op("gpsimd", lambda e: e.memset(mhalf[:], -0.5), writes=["mhalf"])

        small = [(cT_sb, cT_d.rearrange("(k p) s -> p k s", p=128), "cT"), (ngT, ng_d, "ngT"), (caT, ca_d, "caT"),
                 (wsel, wsel_d.rearrange("p (j h q) -> p j h q", j=4, h=2), "wsel"), (cbb, cbb_d, "cbb"), (lng, lng_d, "lng"), (lnb, lnb_d, "lnb")]
        for t, d, key in small:
            T.dma("sync", lambda e, t=t, d=d: e.dma_start(out=t[:], in_=d), writes=[key], semkey=("ld", key))
        bstage = [(fg_bc, "fg_bc"), (gate_bc[0], ("gate_bc", 0)), (gate_bc[1], ("gate_bc", 1))]
        for q, (t, key) in enumerate(bstage):
            T.dma("sync", lambda e, t=t, q=q: e.dma_start(out=t[0:1, :], in_=bada_d[0:1, q * D:(q + 1) * D]),
                  writes=[key], semkey=("ldb", q))

        T.op("scalar", lambda e: e.activation(out=thc[:], in_=cT_sb[:], func=AF.Tanh, scale=0.5), reads=["cT"], writes=["thc"])
        T.op("vector", lambda e: e.scalar_tensor_tensor(out=scT[:], in0=thc[:], scalar=1.0, in1=cT_sb[:], op0=ALU.add, op1=ALU.mult),
             reads=["thc", "cT"], writes=["scT"])
        T.op("vector", lambda e: e.tensor_scalar(out=scT[:], in0=scT[:], scalar1=0.5, scalar2=None, op0=ALU.mult),
             reads=["scT"], writes=["scT"])

        def a_load(i):
            row0 = i * NT
            for sub in range(4):
                xbuf, xkey = XB.next(pin=True)
                T.dma("sync", lambda e, xbuf=xbuf, r=row0 + sub * 128: e.dma_start(out=xbuf[:], in_=x_d[r:r + 128, :]),
                      writes=[xkey], semkey=("xl", xkey[1]))
                a_prep.bufs[sub] = (xbuf, xkey)

        def a_compute(i):
            for sub in range(4):
                xbuf, xkey = a_prep.bufs[sub]
                T.op("scalar", lambda e, xbuf=xbuf, sub=sub: e.activation(out=xs[:, sub, :], in_=xbuf[:], func=AF.Square,
                                                                          accum_out=ssA[:, sub:sub + 1]),
                     reads=[xkey], writes=[("xs", sub), "ssA"])
            T.op("vector", lambda e: e.tensor_scalar(out=msA[:], in0=ssA[:], scalar1=1.0 / D, scalar2=RMS_EPS, op0=ALU.mult, op1=ALU.add),
                 reads=["ssA"], writes=["msA"])
            T.op("gpsimd", lambda e: e.tensor_tensor(out=rsA[:], in0=msA[:], in1=mhalf[:], op=ALU.pow), reads=["msA", "mhalf"], writes=["rsA"])
            for sub in range(4):
                xbuf, xkey = a_prep.bufs[sub]
                T.op("gpsimd", lambda e, xbuf=xbuf, sub=sub: e.tensor_scalar(out=xs[:, sub, :], in0=xbuf[:], scalar1=rsA[:, sub:sub + 1],
                                                                             scalar2=1.0, op0=ALU.mult, op1=ALU.mult),
                     reads=[xkey, "rsA"], writes=[("xs", sub)])
                XB.unpin(xkey)

        def a_prep(i):
            a_load(i)
            a_compute(i)
        a_prep.bufs = [None] * 4

        def mod_thirds(thirds, between):
            banks = {}
            for third in thirds:
                banks[third] = [PB.next(pin=True), PB.next(pin=True)]
            for kc in range(8):
                for third in thirds:
                    xbuf, xkey = XB.next()
                    T.dma("sync", lambda e, xbuf=xbuf, kc=kc, third=third: e.dma_start(
                        out=xbuf[:], in_=wada_d[kc * 128:(kc + 1) * 128, third * D:(third + 1) * D]),
                        writes=[xkey], semkey=("xl", xkey[1]))
                    for h in range(2):
                        bank, bkey = banks[third][h]
                        T.op("tensor", lambda e, bank=bank, xbuf=xbuf, kc=kc, h=h: e.matmul(
                            bank[0:nseq, :], lhsT=scT[:, kc, :], rhs=xbuf[:, h * NT:(h + 1) * NT], start=(kc == 0), stop=False),
                            reads=["scT", xkey], writes=[bkey], signal=(h == 1))
                    between()
            out = {}
            for third in thirds:
                t, key = bstage[third]
                for h in range(2):
                    bank, bkey = banks[third][h]
                    T.op("tensor", lambda e, bank=bank, t=t, h=h: e.matmul(
                        bank[0:nseq, :], lhsT=ones[0:1, 0:nseq], rhs=t[0:1, h * NT:(h + 1) * NT], start=False, stop=True),
                        reads=["ones", key], writes=[bkey])
            for third in thirds:
                mbuf, mkey = XB.next()
                for h in range(2):
                    bank, bkey = banks[third][h]
                    T.op("vector", lambda e, mbuf=mbuf, bank=bank, h=h: e.tensor_copy(
                        out=mbuf[0:nseq, h * NT:(h + 1) * NT], in_=bank[0:nseq, :]), reads=[bkey], writes=[mkey])
                    PB.unpin(bkey)
                out[third] = (mbuf, mkey)
            return out

        first_pieces = [("in", blk, kc) for blk in (5, 4, 0, 2) for kc in range(8)]

        def between():
            for _ in range(2):
                if first_pieces:
                    load_w_piece(*first_pieces.pop(0))

        a_prep(0)
        mb = mod_thirds([0, 1], between)
        while first_pieces:
            load_w_piece(*first_pieces.pop(0))
        modbuf = [mb[0], mb[1], None]
        T.dma("sync", lambda e: e.dma_start(out=fg_bc[:], in_=fg_d.partition_broadcast(128)), writes=["fg_bc"], semkey="ld_fg")

        def gate_part():
            g = mod_thirds([2], lambda: None)[2]
            T.dma("sync", lambda e: e.dma_start(out=gate_d, in_=g[0][0:nseq, :]), reads=[g[1]], writes=["gate_d"], semkey="st_gate")
            load_gate(0)

        bank_sh, bkey_sh = PB.next()
        bank_sc, bkey_sc = PB.next()
        for kc in range(8):
            T.op("tensor", lambda e, kc=kc: e.transpose(out=bank_sh[:, kc * nseq:(kc + 1) * nseq],
                                                        in_=modbuf[0][0][0:nseq, kc * 128:(kc + 1) * 128],
                                                        identity=identf[0:nseq, 0:nseq]),
                 reads=[modbuf[0][1], "identf"], writes=[bkey_sh], signal=(kc == 7))
        for kc in range(8):
            T.op("tensor", lambda e, kc=kc: e.transpose(out=bank_sc[:, kc * nseq:(kc + 1) * nseq],
                                                        in_=modbuf[1][0][0:nseq, kc * 128:(kc + 1) * 128],
                                                        identity=identf[0:nseq, 0:nseq]),
                 reads=[modbuf[1][1], "identf"], writes=[bkey_sc], signal=(kc == 7))
        T.op("vector", lambda e: e.tensor_copy(out=shT[:].rearrange("p k s -> p (k s)"), in_=bank_sh[:, 0:8 * nseq]),
             reads=[bkey_sh], writes=["shT"])
        for kc in range(8):
            T.op("vector", lambda e, kc=kc: e.tensor_scalar(out=gsT[:, kc, :], in0=bank_sc[:, kc * nseq:(kc + 1) * nseq],
                                                            scalar1=1.0, scalar2=ngT[:, kc:kc + 1], op0=ALU.add, op1=ALU.mult),
                 reads=[bkey_sc, "ngT"], writes=["gsT"])

        bank_cb, bkey_cb = PB.next()
        T.op("tensor", lambda e: e.matmul(bank_cb[:, 0:4], lhsT=Cm[:], rhs=cbb[:], start=True, stop=True),
             reads=["Cm", "cbb"], writes=[bkey_cb])
        T.op("vector", lambda e: e.tensor_copy(out=cbs[:], in_=bank_cb[:, 0:4]), reads=[bkey_cb], writes=["cbs"])
        T.op("vector", lambda e: e.tensor_scalar(out=caH[:], in0=caT[:], scalar1=0.5, scalar2=None, op0=ALU.mult), reads=["caT"], writes=["caH"])
        T.op("vector", lambda e: e.tensor_scalar(out=qg[:], in0=lng[:], scalar1=0.25, scalar2=None, op0=ALU.mult), reads=["lng"], writes=["qg"])
        T.op("vector", lambda e: e.tensor_scalar(out=qb[:], in0=lnb[:], scalar1=0.25, scalar2=None, op0=ALU.mult), reads=["lnb"], writes=["qb"])

        def build_conv_w():
            n_w = 0
            for j in range(4):
                for h in range(2):
                    for q in range(16):
                        eng = "vector" if (n_w % 2 == 0) else "gpsimd"
                        n_w += 1
                        T.op(eng, lambda e, j=j, h=h, q=q: e.tensor_scalar(out=WC[:, j, h, q, :], in0=Cc2[:], scalar1=wsel[:, j, h, q:q + 1],
                                                                           scalar2=0.5, op0=ALU.mult, op1=ALU.mult),
                             reads=["Cc2", "wsel"], writes=[("WC", j, h, q)])

        def load_gate(s):
            g = gate_bc[s % 2]
            T.dma("sync", lambda e, g=g, s=s: e.dma_start(out=g[:], in_=gate_d[s:s + 1, :].partition_broadcast(128)),
                  reads=["gate_d"], writes=[("gate_bc", s % 2)], semkey=("ld_gate", s % 2))

        def a_T(i):
            for kc in range(8):
                a_T_group(i, kc)

        def a_T_group(i, kc):
            s = tile_seq[i]
            if True:
                pt, pkey = PTR.next()
                for sub in range(4):
                    T.op("tensor", lambda e, pt=pt, sub=sub, kc=kc: e.transpose(out=pt[:, sub * 128:(sub + 1) * 128],
                                                                                in_=xs[:, sub, kc * 128:(kc + 1) * 128],
                                                                                identity=ident[:]),
                         reads=[("xs", sub), "ident"], writes=[pkey], signal=(sub == 3))
                if kc % 2 == 0:
                    T.op("scalar", lambda e, pt=pt, kc=kc, s=s: e.activation(out=hT[:, kc, :], in_=pt, func=AF.Identity,
                                                                            scale=gsT[:, kc, s:s + 1], bias=shT[:, kc, s:s + 1]),
                         reads=[pkey, "gsT", "shT"], writes=[("hT", kc)])
                else:
                    T.op("vector", lambda e, pt=pt, kc=kc, s=s: e.tensor_scalar(out=hT[:, kc, :], in0=pt, scalar1=gsT[:, kc, s:s + 1],
                                                                               scalar2=shT[:, kc, s:s + 1], op0=ALU.mult, op1=ALU.add),
                         reads=[pkey, "gsT", "shT"], writes=[("hT", kc)])

        hT_keys = [("hT", kc) for kc in range(8)]

        def mm1(col0):
            bank, bkey = PB.next()
            for kc in range(8):
                T.op("tensor", lambda e, bank=bank, kc=kc: e.matmul(bank[:], lhsT=w_in_sb[:, kc, col0:col0 + 128], rhs=hT[:, kc, :],
                                                                    start=(kc == 0), stop=(kc == 7)),
                     reads=[("w_in", col0 // NT), ("hT", kc)], writes=[bkey], signal=(kc == 7))
            return bank, bkey

        def b1(i):
            slot = i % 2
            u, t1 = u_sb[slot], t1_sb[slot]
            for j in range(4):
                bank, bkey = mm1(2560 + j * 128)
                th, tkey = TP.next()
                T.op("scalar", lambda e, th=th, bank=bank: e.activation(out=th[:], in_=bank[:], func=AF.Tanh, scale=0.5),
                     reads=[bkey], writes=[tkey])
                bank2, bkey2 = mm1(2048 + j * 128)
                for h in range(2):
                    T.op("vector", lambda e, th=th, bank2=bank2, j=j, h=h: e.scalar_tensor_tensor(
                        out=u[64 * h:64 * h + 64, j, h, 15:15 + NT], in0=th[64 * h:64 * h + 64, :], scalar=1.0,
                        in1=bank2[64 * h:64 * h + 64, :], op0=ALU.add, op1=ALU.mult),
                        reads=[tkey, bkey2], writes=[("u", slot)])
            for j in range(4):
                bank, bkey = mm1(0 + j * 128)
                ain, akey = TP.next()
                T.op("scalar", lambda e, ain=ain, bank=bank: e.activation(out=ain[:], in_=bank[:], func=AF.Copy), reads=[bkey], writes=[akey])
                bank2, bkey2 = mm1(1024 + j * 128)
                T.op("vector", lambda e, ain=ain, bank2=bank2, j=j: e.tensor_tensor(out=t1[:, j, 1:1 + NT], in0=bank2[:], in1=ain[:], op=ALU.mult),
                     reads=[akey, bkey2], writes=[("t1", slot)])
            up, t1p = u_sb[1 - slot], t1_sb[1 - slot]
            if tile_first[i]:
                T.op("gpsimd", lambda e: e.memset(u[:, :, :, 0:15], 0.0), writes=[("u", slot)])
                T.op("gpsimd", lambda e: e.memset(t1[:, :, 0:1], 0.0), writes=[("t1", slot)])
            else:
                T.op("gpsimd", lambda e: e.tensor_copy(out=u[:, :, :, 0:15], in_=up[:, :, :, NT:NT + 15]), reads=[("u", 1 - slot)], writes=[("u", slot)])
                T.op("gpsimd", lambda e: e.tensor_copy(out=up[:, :, :, NT + 15:NT + 30], in_=u[:, :, :, 15:30]), reads=[("u", slot)], writes=[("u", 1 - slot)])
                T.op("gpsimd", lambda e: e.tensor_copy(out=t1[:, :, 0:1], in_=t1p[:, :, NT:NT + 1]), reads=[("t1", 1 - slot)], writes=[("t1", slot)])
                T.op("gpsimd", lambda e: e.tensor_copy(out=t1p[:, :, NT + 1:NT + 2], in_=t1[:, :, 1:2]), reads=[("t1", slot)], writes=[("t1", 1 - slot)])
            if tile_last[i]:
                T.op("gpsimd", lambda e: e.memset(u[:, :, :, NT + 15:NT + 30], 0.0), writes=[("u", slot)])
                T.op("gpsimd", lambda e: e.memset(t1[:, :, NT + 1:NT + 2], 0.0), writes=[("t1", slot)])

        def b2_a(j):
            bank, bkey = mm1(1536 + j * 128)
            th, tkey = TP.next()
            T.op("scalar", lambda e, th=th, bank=bank: e.activation(out=th[:], in_=bank[:], func=AF.Tanh, scale=0.5), reads=[bkey], writes=[tkey])
            T.op("vector", lambda e, th=th, bank=bank: e.scalar_tensor_tensor(out=th[:], in0=th[:], scalar=1.0, in1=bank[:], op0=ALU.add, op1=ALU.mult),
                 reads=[tkey, bkey], writes=[tkey])
            bank2, bkey2 = mm1(512 + j * 128)
            T.op("vector", lambda e, th=th, bank2=bank2, j=j: e.tensor_tensor(out=ga[:, j, :], in0=bank2[:], in1=th[:], op=ALU.mult),
                 reads=[tkey, bkey2], writes=[("ga", j)])

        def gbuf(i, j):
            if j < 2 or i % 2 == 0:
                return gb[:, j, :], ("gb", j)
            return gb2[:, j - 2, :], ("gb2", j)

        def b2_b(i, j):
            gdst, gkey = gbuf(i, j)
            bank, bkey = mm1(3072 + j * 128)
            th, tkey = TP.next()
            T.op("scalar", lambda e, th=th, bank=bank: e.activation(out=th[:], in_=bank[:], func=AF.Tanh, scale=0.5), reads=[bkey], writes=[tkey])
            T.op("vector", lambda e, th=th, bank=bank: e.scalar_tensor_tensor(out=gdst, in0=th[:], scalar=1.0, in1=bank[:], op0=ALU.add, op1=ALU.mult),
                 reads=[tkey, bkey], writes=[gkey])

        def conv_a(i, j):
            slot = i % 2
            t1 = t1_sb[slot]
            acc, akey = TP.next()
            T.op("vector", lambda e: e.tensor_scalar(out=acc[:], in0=t1[:, j, 0:NT], scalar1=caH[:, j * KA:j * KA + 1], scalar2=None, op0=ALU.mult),
                 reads=[("t1", slot), "caH"], writes=[akey])
            for k in (1, 2):
                T.op("vector", lambda e, k=k: e.scalar_tensor_tensor(out=acc[:], in0=t1[:, j, k:k + NT], scalar=caH[:, j * KA + k:j * KA + k + 1],
                                                                   in1=acc[:], op0=ALU.mult, op1=ALU.add),
                     reads=[("t1", slot), "caH", akey], writes=[akey])
            T.op("vector", lambda e: e.tensor_tensor(out=yT[:, j, :], in0=acc[:], in1=ga[:, j, :], op=ALU.mult),
                 reads=[akey, ("ga", j)], writes=[("yT", j)])

        def us_dma(i):
            slot = i % 2
            u = u_sb[slot]
            for h in range(2):
                a, b = 64 * h, 64 * (1 - h)
                for j in range(4):
                    T.dma("sync", lambda e, a=a, b=b, h=h, j=j: e.dma_start(out=u[b:b + 64, j, h, 0:NT + 29], in_=u[a:a + 64, j, h, 1:NT + 30]),
                          reads=[("u", slot)], writes=[("us", slot, j)], semkey=("usd", slot, h, j))

        def conv_b(i, j):
            slot = i % 2
            u = u_sb[slot]
            bank, bkey = PB.next()
            for q in range(16):
                for h in range(2):
                    T.op("tensor", lambda e, bank=bank, q=q, h=h: e.matmul(
                        bank[64 * h:64 * h + 64, :], lhsT=WC[:, j, h, q, :], rhs=u[:, j, h, 2 * q:2 * q + NT],
                        start=(q == 0), stop=(q == 15), tile_position=(0, 64 * h)),
                        reads=[("WC", j, h, q), ("u", slot), ("us", slot, j)], writes=[bkey], signal=(q == 15 and h == 1))
            sq, skey = sqr.next()
            T.op("scalar", lambda e, sq=sq, bank=bank: e.activation(out=sq[:], in_=bank[:], func=AF.Square, bias=cbs[:, j:j + 1], scale=1.0),
                 reads=[bkey, "cbs"], writes=[skey])
            cc, ckey = TP.next(pin=True)
            T.op("scalar", lambda e, cc=cc, bank=bank: e.activation(out=cc[:], in_=bank[:], func=AF.Identity, bias=cbs[:, j:j + 1], scale=1.0),
                 reads=[bkey, "cbs"], writes=[ckey])
            return cc, ckey, sq, skey

        def ln_pair(ti, items):
            st_ = []
            for (j, cc, ckey, sq, skey) in items:
                vb, vkey = PB.next()
                T.op("tensor", lambda e, vb=vb, sq=sq: e.matmul(vb[:], lhsT=Bm[:], rhs=sq[:], start=True, stop=True), reads=["Bm", skey], writes=[vkey])
                r, rkey = TP.next(pin=True)
                st_.append((j, cc, ckey, vb, vkey, r, rkey))
            T.op("scalar", lambda e: e.activation(out=dmy[:, 0:1], in_=ones[:, 0:1], func=AF.Ln), reads=["ones"], writes=["dmy0"])
            for (j, cc, ckey, vb, vkey, r, rkey) in st_:
                T.op("scalar", lambda e, r=r, vb=vb: e.activation(out=r[:], in_=vb[:], func=AF.Ln, bias=LN_EPS, scale=1.0), reads=[vkey], writes=[rkey])
                T.op("scalar", lambda e, r=r: e.activation(out=r[:], in_=r[:], func=AF.Exp, scale=-0.5), reads=[rkey], writes=[rkey])
            T.op("scalar", lambda e: e.activation(out=dmy[:, 1:2], in_=ones[:, 0:1], func=AF.Tanh), reads=["ones"], writes=["dmy1"])
            for (j, cc, ckey, vb, vkey, r, rkey) in st_:
                T.op("vector", lambda e, r=r, cc=cc: e.tensor_tensor(out=r[:], in0=cc[:], in1=r[:], op=ALU.mult),
                     reads=[ckey, rkey], writes=[rkey])
                T.op("vector", lambda e, r=r, j=j: e.tensor_scalar(out=r[:], in0=r[:], scalar1=qg[:, j:j + 1], scalar2=qb[:, j:j + 1], op0=ALU.mult, op1=ALU.add),
                     reads=[rkey, "qg", "qb"], writes=[rkey])
            for (j, cc, ckey, vb, vkey, r, rkey) in st_:
                T.op("scalar", lambda e, cc=cc, r=r: e.activation(out=cc[:], in_=r[:], func=AF.Tanh, scale=2.0), reads=[rkey], writes=[ckey])
            for (j, cc, ckey, vb, vkey, r, rkey) in st_:
                T.op("vector", lambda e, cc=cc, r=r: e.scalar_tensor_tensor(out=cc[:], in0=cc[:], scalar=1.0, in1=r[:], op0=ALU.add, op1=ALU.mult),
                     reads=[ckey, rkey], writes=[ckey])
                TP.unpin(rkey)
                gsrc, gkey = gbuf(ti, j)
                T.op("gpsimd", lambda e, cc=cc, j=j, gsrc=gsrc: e.tensor_tensor(out=yT[:, 4 + j, :], in0=cc[:], in1=gsrc, op=ALU.mult),
                     reads=[ckey, gkey], writes=[("yT", 4 + j)])
                TP.unpin(ckey)

        def d_loads(i):
            row0 = i * NT
            for sub in range(4):
                xbuf, xkey = XB.next(pin=True)
                T.dma("sync", lambda e, xbuf=xbuf, r=row0 + sub * 128: e.dma_start(out=xbuf[:], in_=x_d[r:r + 128, :]),
                      writes=[xkey], semkey=("xl", xkey[1]))
                d_loads.bufs[sub] = (xbuf, xkey)
        d_loads.bufs = [None] * 4

        def stage_d(i, nxt=None):
            s = tile_seq[i]
            g = gate_bc[s % 2]
            gkey = ("gate_bc", s % 2)
            row0 = i * NT
            fin = []
            for sub in range(4):
                xbuf, xkey = d_loads.bufs[sub]
                for h in range(2):
                    if nxt is not None and sub < 2:
                        a_T_group(nxt, (sub * 2 + h) * 2)
                        a_T_group(nxt, (sub * 2 + h) * 2 + 1)
                    bank, bkey = PB.next()
                    for kc in range(8):
                        T.op("tensor", lambda e, bank=bank, kc=kc, sub=sub, h=h: e.matmul(
                            bank[:], lhsT=yT[:, kc, sub * 128:(sub + 1) * 128], rhs=w_out_sb[:, kc, h * NT:(h + 1) * NT],
                            start=(kc == 0), stop=(kc == 7)),
                            reads=[("yT", kc), ("w_out", h)], writes=[bkey], signal=(kc == 7))
                    tmp, tkey = TP.next()
                    T.op("vector", lambda e, tmp=tmp, bank=bank, h=h: e.tensor_tensor(out=tmp[:], in0=bank[:], in1=g[:, h * NT:(h + 1) * NT], op=ALU.mult),
                         reads=[bkey, gkey], writes=[tkey])
                    T.op("gpsimd", lambda e, tmp=tmp, xbuf=xbuf, h=h: e.tensor_tensor(out=xbuf[:, h * NT:(h + 1) * NT], in0=tmp[:],
                                                                                      in1=xbuf[:, h * NT:(h + 1) * NT], op=ALU.add),
                         reads=[tkey, xkey], writes=[xkey])
                fin.append((xbuf, xkey, row0 + sub * 128))
                if len(fin) > 1:
                    d_finish(*fin.pop(0))
            while fin:
                d_finish(*fin.pop(0))

        def d_finish(xbuf, xkey, r):
            c = cur["eidx"] % 8
            cur["eidx"] += 1
            jb, jkey = TP.next()
            T.op("scalar", lambda e: e.activation(out=jb[:].bitcast(BF16), in_=xbuf[:], func=AF.Square, accum_out=ssE[:, c:c + 1]),
                 reads=[xkey], writes=[jkey, ("ssE", c)])
            T.op("vector", lambda e: e.tensor_scalar(out=msE[:, c:c + 1], in0=ssE[:, c:c + 1], scalar1=1.0 / D, scalar2=RMS_EPS,
                                                     op0=ALU.mult, op1=ALU.add), reads=[("ssE", c)], writes=[("msE", c)])
            T.op("gpsimd", lambda e: e.tensor_tensor(out=rsE[:, c:c + 1], in0=msE[:, c:c + 1], in1=mhalf[:, 0:1], op=ALU.pow),
                 reads=[("msE", c), "mhalf"], writes=[("rsE", c)])
            T.op("vector", lambda e: e.scalar_tensor_tensor(out=xbuf[:], in0=xbuf[:], scalar=rsE[:, c:c + 1], in1=fg_bc[:],
                                                            op0=ALU.mult, op1=ALU.mult),
                 reads=[xkey, ("rsE", c), "fg_bc"], writes=[xkey])
            T.dma("sync", lambda e: e.dma_start(out=y_d[r:r + 128, :], in_=xbuf[:]),
                  reads=[xkey], writes=[("ydram", xkey[1])], semkey=("xst", xkey[1]))
            XB.unpin(xkey)

        a_T(0)
        for i in range(ntiles + 1):
            cur_ok = i < ntiles
            prev_ok = i >= 1
            pend = None
            if cur_ok:
                if tile_first[i] and tile_seq[i] > 0:
                    load_gate(tile_seq[i])
                b1(i)
                if i == 0:
                    build_conv_w()
                    for blk in (3, 1, 6):
                        load_w_block("in", blk)
                if i + 1 < ntiles:
                    a_load(i + 1)
                if i == 0:
                    load_w_block("out", 0)
                    load_w_block("out", 1)
            if prev_ok:
                us_dma(i - 1)
                d_loads(i - 1)
                conv_a(i - 1, 0)
                conv_a(i - 1, 1)
                c0 = (0,) + conv_b(i - 1, 0)
                T.op("scalar", lambda e: e.activation(out=dmy[:, 0:1], in_=ones[:, 0:1], func=AF.Ln), reads=["ones"], writes=["dmy0"])
                c1 = (1,) + conv_b(i - 1, 1)
                conv_a(i - 1, 2)
                conv_a(i - 1, 3)
                ln_pair(i - 1, [c0, c1])
                c2 = (2,) + conv_b(i - 1, 2)
            if cur_ok:
                b2_a(0)
            if prev_ok:
                c3 = (3,) + conv_b(i - 1, 3)
            if cur_ok:
                b2_a(1)
            if prev_ok:
                ln_pair(i - 1, [c2, c3])
            if cur_ok and i + 1 < ntiles:
                a_compute(i + 1)
            if cur_ok:
                b2_a(2)
                b2_a(3)
                for j in range(4):
                    b2_b(i, j)
            if i == 0:
                gate_part()
            nxt = i + 1 if i + 1 < ntiles else None
            if prev_ok:
                stage_d(i - 1, nxt)
            elif nxt is not None:
                a_T(nxt)
        T._need("sync", [(k, v) for k, v in T.count.items() if isinstance(k, tuple) and k[0] == "xst"])
        T.emit()
    return nc


def _core_inputs(k, x_prompt, x_sample, c_prompt, c_sample, shared):
    xs4 = x_sample[4 * k:4 * k + 4].reshape(-1, D)
    x = np.concatenate([x_prompt[k], xs4], axis=0)
    cT = np.ascontiguousarray(np.concatenate([c_prompt[k][None], c_sample[4 * k:4 * k + 4]], axis=0).T)
    m = {"x": np.ascontiguousarray(x, dtype=np.float32), "cT": cT.astype(np.float32)}
    m.update(shared)
    return m


def _wsel(w):
    out = np.zeros((128, 4, 2, 16), np.float32)
    p = np.arange(128)
    for j in range(4):
        for h in range(2):
            nat = (p // 64) == h
            for q in range(16):
                out[nat, j, h, q] = w[2 * q, j * 128 + p[nat]]
                if 2 * q + 1 < KB:
                    out[~nat, j, h, q] = w[2 * q + 1, j * 128 + (p[~nat] + 64) % 128]
    return np.ascontiguousarray(out.reshape(128, 128))


def _shared(norm_g, w_ada, b_ada, w_in, conv_a_w, conv_b_w, conv_b_b, ln_g, ln_b, w_out, final_g):
    f = lambda a: np.ascontiguousarray(np.asarray(a, dtype=np.float32))
    return {
        "ngT": f(np.asarray(norm_g)[0].reshape(8, 128).T),
        "w_ada": f(np.asarray(w_ada)[0]),
        "b_ada": f(np.asarray(b_ada)[0].reshape(1, 3 * D)),
        "w_in": f(np.asarray(w_in)[0]),
        "caT": f(np.asarray(conv_a_w)[0].reshape(KA, 4, 128).transpose(2, 1, 0).reshape(128, 4 * KA)),
        "wsel": _wsel(np.asarray(conv_b_w, dtype=np.float32)[0]),
        "cbbT": f(np.asarray(conv_b_b)[0].reshape(4, 128).T),
        "lngT": f(np.asarray(ln_g)[0].reshape(4, 128).T),
        "lnbT": f(np.asarray(ln_b)[0].reshape(4, 128).T),
        "w_out": f(np.asarray(w_out)[0]),
        "fg": f(np.asarray(final_g).reshape(1, D)),
    }


def kernel(x_prompt, x_sample, c_prompt, c_sample, norm_g, w_ada, b_ada, w_in,
           conv_a_w, conv_b_w, conv_b_b, ln_g, ln_b, w_out, final_g):
    x_prompt = np.asarray(x_prompt, dtype=np.float32)
    x_sample = np.asarray(x_sample, dtype=np.float32)
    c_prompt = np.asarray(c_prompt, dtype=np.float32)
    c_sample = np.asarray(c_sample, dtype=np.float32)
    shared = _shared(norm_g, w_ada, b_ada, w_in, conv_a_w, conv_b_w, conv_b_b, ln_g, ln_b, w_out, final_g)
    nc = build_nc(SEQ_TILES)
    in_maps = [_core_inputs(k, x_prompt, x_sample, c_prompt, c_sample, shared) for k in range(N_CORES)]
    res = run_bass_kernel_spmd(nc, in_maps, core_ids=list(range(N_CORES)))
    y_prompt = np.empty_like(x_prompt)
    y_sample = np.empty_like(x_sample)
    for k in range(N_CORES):
        y = res.results[k]["y"]
        y_prompt[k] = y[:8192]
        y_sample[4 * k:4 * k + 4] = y[8192:].reshape(4, 2048, D)
    return (y_prompt, y_sample)
```
